# Optimizing a Trainium2 kernel written in Bass

```python
import math
import jax, jax.numpy as jnp
from jax import lax
import numpy as np

D_MODEL = 1024
BATCH = 16
SEQ = 2048
DEPTH = 1

D_CONV = D_MODEL // 2
D_ATTN = D_MODEL - D_CONV
HEAD_DIM = 64
N_HEADS = D_ATTN // HEAD_DIM
D_MIX = D_CONV + D_ATTN
D_IN_PROJ = 2 * D_CONV + 3 * D_ATTN
CONV_WIDTH = 31
DILATED_PATTERNS = ((128, 1), (512, 4), (2048, 16))
ATTN_BLOCK = 128
PEER_HEADS = 8
PEER_NKEYS = 128
PEER_QDIM = 256
PEER_TOPK = 16
PEER_CHUNK = 128
N_EXPERTS = PEER_NKEYS * PEER_NKEYS
ALPHA = (2.0 * DEPTH) ** 0.25
BETA = (8.0 * DEPTH) ** -0.25
LN_EPS = 1e-5
NEG_INF = -1e30

kernel_name = "hymba_conformer_dilated_alibi_peer_deepnorm"


def layer_norm(x, g, b):
    xf = x.astype(jnp.float32)
    mu = jnp.mean(xf, axis=-1, keepdims=True)
    var = jnp.mean(jnp.square(xf - mu), axis=-1, keepdims=True)
    return ((xf - mu) * lax.rsqrt(var + LN_EPS)).astype(x.dtype) * g + b


def alibi_slopes():
    return 2.0 ** (-8.0 * jnp.arange(1, N_HEADS + 1, dtype=jnp.float32) / N_HEADS)


def conformer_conv(u, w_dw, b_dw, g, b):
    a, gate = jnp.split(u, 2, axis=-1)
    hdn = a * jax.nn.sigmoid(gate)
    hdn = lax.conv_general_dilated(
        hdn, w_dw[:, None, :], window_strides=(1,),
        padding=((CONV_WIDTH - 1, 0),),
        dimension_numbers=("NWC", "WIO", "NWC"),
        feature_group_count=D_CONV) + b_dw
    hdn = layer_norm(hdn, g, b)
    return jax.nn.silu(hdn)


def dilated_branch(q, k, v, slopes, window, dilation):
    B, S, H, dh = q.shape
    L = S // dilation
    n_back = window // dilation
    nb = -(-L // ATTN_BLOCK)
    pad = nb * ATTN_BLOCK - L
    n_prev = -(-n_back // ATTN_BLOCK)
    kw = (n_prev + 1) * ATTN_BLOCK

    def to_stream(t):
        t = t.reshape(B, L, dilation, H, dh).transpose(0, 2, 3, 1, 4)
        t = jnp.pad(t, ((0, 0), (0, 0), (0, 0), (0, pad), (0, 0)))
        return t.reshape(B, dilation, H, nb, ATTN_BLOCK, dh)

    def band(t):
        tp = jnp.pad(t, ((0, 0), (0, 0), (0, 0), (n_prev, 0), (0, 0), (0, 0)))
        return jnp.concatenate([tp[:, :, :, i:i + nb] for i in range(n_prev + 1)], axis=4)

    qb = to_stream(q * (HEAD_DIM ** -0.5))
    kb = band(to_stream(k))
    vb = band(to_stream(v))

    q_loc = n_prev * ATTN_BLOCK + jnp.arange(ATTN_BLOCK)[:, None]
    k_loc = jnp.arange(kw)[None, :]
    dist = q_loc - k_loc
    j_glob = (jnp.arange(nb)[:, None, None] - n_prev) * ATTN_BLOCK + k_loc[None]
    valid = ((dist >= 0) & (dist <= n_back))[None] & (j_glob >= 0)
    bias = -slopes[:, None, None, None] * (dist * dilation).astype(jnp.float32)[None, None]

    s = jnp.einsum("bdhnqc,bdhnkc->bdhnqk", qb, kb, preferred_element_type=jnp.float32) + bias
    s = jnp.where(valid, s, NEG_INF)
    m = jnp.max(s, axis=-1, keepdims=True)
    p = jnp.exp(s - m)
    l = jnp.sum(p, axis=-1, keepdims=True)
    o = jnp.einsum("bdhnqk,bdhnkc->bdhnqc", p, vb.astype(jnp.float32)) / l
    lse = (m + jnp.log(l))[..., 0]

    o = o.reshape(B, dilation, H, nb * ATTN_BLOCK, dh)[:, :, :, :L]
    o = o.transpose(0, 3, 1, 2, 4).reshape(B, S, H, dh)
    lse = lse.reshape(B, dilation, H, nb * ATTN_BLOCK)[:, :, :, :L]
    lse = lse.transpose(0, 3, 1, 2).reshape(B, S, H)
    return o, lse


def dilated_attention(q, k, v):
    slopes = alibi_slopes()
    outs, lses = [], []
    for window, dilation in DILATED_PATTERNS:
        o, lse = dilated_branch(q, k, v, slopes, window, dilation)
        outs.append(o)
        lses.append(lse)
    w = jax.nn.softmax(jnp.stack(lses, axis=0), axis=0)
    out = jnp.sum(w[..., None] * jnp.stack(outs, axis=0), axis=0)
    return out.astype(q.dtype)


def peer(h, w_q, sub_keys, u_tab, v_tab):
    B, S, D = h.shape
    tokens = h.reshape(-1, PEER_CHUNK, D)

    def chunk_fn(xc):
        q = (xc @ w_q).reshape(PEER_CHUNK, PEER_HEADS, 2, PEER_QDIM // 2)
        s = jnp.einsum("thpc,hpnc->thpn", q, sub_keys, preferred_element_type=jnp.float32)
        top_s, top_i = lax.top_k(s, PEER_TOPK)
        cand_s = top_s[:, :, 0, :, None] + top_s[:, :, 1, None, :]
        cand_i = top_i[:, :, 0, :, None] * PEER_NKEYS + top_i[:, :, 1, None, :]
        cand_s = cand_s.reshape(PEER_CHUNK, PEER_HEADS, PEER_TOPK * PEER_TOPK)
        cand_i = cand_i.reshape(PEER_CHUNK, PEER_HEADS, PEER_TOPK * PEER_TOPK)
        best_s, best_pos = lax.top_k(cand_s, PEER_TOPK)
        idx = jnp.take_along_axis(cand_i, best_pos, axis=-1)
        g = jax.nn.softmax(best_s, axis=-1)
        u = u_tab[idx]
        act = jax.nn.gelu(jnp.einsum("thkd,td->thk", u, xc), approximate=False)
        v = v_tab[idx]
        return jnp.einsum("thk,thkd->td", (g * act).astype(xc.dtype), v)

    return lax.map(chunk_fn, tokens).reshape(B, S, D)


def setup_inputs(seed: int = 0) -> dict:
    key = jax.random.key(seed)
    ks = jax.random.split(key, 20)
    f32 = jnp.float32
    nrm = lambda k, shape: jax.random.normal(k, shape, dtype=f32)
    return {
        "x": nrm(ks[0], (BATCH, SEQ, D_MODEL)),
        "w_in": nrm(ks[1], (DEPTH, D_MODEL, D_IN_PROJ)) * D_MODEL ** -0.5,
        "b_in": nrm(ks[2], (DEPTH, D_IN_PROJ)) * 0.01,
        "conv_w": nrm(ks[3], (DEPTH, CONV_WIDTH, D_CONV)) * CONV_WIDTH ** -0.5,
        "conv_b": nrm(ks[4], (DEPTH, D_CONV)) * 0.01,
        "conv_ln_g": 1.0 + 0.01 * nrm(ks[5], (DEPTH, D_CONV)),
        "conv_ln_b": nrm(ks[6], (DEPTH, D_CONV)) * 0.01,
        "w_out": nrm(ks[7], (DEPTH, D_MIX, D_MODEL)) * (D_MIX ** -0.5) * BETA,
        "b_out": nrm(ks[8], (DEPTH, D_MODEL)) * 0.01,
        "ln1_g": 1.0 + 0.01 * nrm(ks[9], (DEPTH, D_MODEL)),
        "ln1_b": nrm(ks[10], (DEPTH, D_MODEL)) * 0.01,
        "peer_wq": nrm(ks[11], (DEPTH, D_MODEL, PEER_HEADS * PEER_QDIM)) * D_MODEL ** -0.5,
        "peer_keys": nrm(ks[12], (DEPTH, PEER_HEADS, 2, PEER_NKEYS, PEER_QDIM // 2)) * (PEER_QDIM // 2) ** -0.5,
        "peer_u": nrm(ks[13], (DEPTH, N_EXPERTS, D_MODEL)) * D_MODEL ** -0.5,
        "peer_v": nrm(ks[14], (DEPTH, N_EXPERTS, D_MODEL)) * BETA * PEER_HEADS ** -0.5,
        "ln2_g": 1.0 + 0.01 * nrm(ks[15], (DEPTH, D_MODEL)),
        "ln2_b": nrm(ks[16], (DEPTH, D_MODEL)) * 0.01,
    }


def reference(x, w_in, b_in, conv_w, conv_b, conv_ln_g, conv_ln_b, w_out, b_out,
              ln1_g, ln1_b, peer_wq, peer_keys, peer_u, peer_v, ln2_g, ln2_b):
    B, S, _ = x.shape
    h = x
    for l in range(DEPTH):
        proj = h @ w_in[l] + b_in[l]
        u_conv, q, k, v = jnp.split(proj, [2 * D_CONV, 2 * D_CONV + D_ATTN, 2 * D_CONV + 2 * D_ATTN], axis=-1)
        y_conv = conformer_conv(u_conv, conv_w[l], conv_b[l], conv_ln_g[l], conv_ln_b[l])
        q = q.reshape(B, S, N_HEADS, HEAD_DIM)
        k = k.reshape(B, S, N_HEADS, HEAD_DIM)
        v = v.reshape(B, S, N_HEADS, HEAD_DIM)
        y_attn = dilated_attention(q, k, v).reshape(B, S, D_ATTN)
        y = jnp.concatenate([y_conv, y_attn], axis=-1) @ w_out[l] + b_out[l]
        h = layer_norm(ALPHA * h + y, ln1_g[l], ln1_b[l])
        y = peer(h, peer_wq[l], peer_keys[l], peer_u[l], peer_v[l])
        h = layer_norm(ALPHA * h + y, ln2_g[l], ln2_b[l])
    return h
```

```python
import numpy as np
import ml_dtypes
from contextlib import ExitStack
import concourse.bass as bass
import concourse.mybir as mybir
from concourse.bass_utils import run_bass_kernel_spmd

F32 = mybir.dt.float32
BF16 = mybir.dt.bfloat16
I32 = mybir.dt.int32
U32 = mybir.dt.uint32
ALU = mybir.AluOpType
AF = mybir.ActivationFunctionType
AX = mybir.AxisListType

D = 1024
SEQ = 2048
NT = SEQ // 128
ALPHA = 2.0 ** 0.25
LN_EPS = 1e-5
NG = 14
N_CORES = 8


class Buf:
    def __init__(self, name):
        self.name = name
        self.w = None
        self.r = {}


class Prog:
    ENG = ("pe", "act", "dve", "pool", "sp")

    def __init__(self, nc, stack):
        self.nc = nc
        self.stack = stack
        self.q = {e: [] for e in self.ENG}
        self.sems = {}
        self.cnt = {}
        self.seen = {e: {} for e in self.ENG}
        self.epoch = {e: 0 for e in self.ENG}
        self.ecount = {e: 0 for e in self.ENG}
        self.pend = {e: [] for e in self.ENG}
        self.rec = None

    def replay(self, item):
        if item is None:
            return
        kind, args = item
        rec, self.rec = self.rec, None
        try:
            if kind == "op":
                self.op(*args)
            else:
                self.dma(*args)
        finally:
            self.rec = rec

    def sem(self, name):
        if name not in self.sems:
            self.sems[name] = self.stack.enter_context(self.nc.semaphore(name))
            self.cnt[name] = 0
        return self.sems[name]

    def _waits(self, eng, waits):
        for w in waits:
            if w is None:
                continue
            s, v = w
            if eng == "pe" and s.startswith("c_pe_"):
                continue
            if self.seen[eng].get(s, 0) >= v:
                continue
            self.seen[eng][s] = v
            h = self.sem(s)
            self.q[eng].append(lambda e, h=h, v=v: e.wait_ge(h, v))

    def _deps(self, reads, writes, waits):
        ws = list(waits)
        for b in reads:
            ws.append(b.w)
        for b in writes:
            ws.append(b.w)
            ws.extend(b.r.values())
        return ws

    def _finish(self, key, tok, reads, writes):
        for b in self.pend[key] if key in self.pend else ():
            b.r[key] = tok
        if key in self.pend:
            self.pend[key] = []
        for b in reads:
            b.r[key] = tok
        for b in writes:
            b.w = tok
            b.r = {}

    def op(self, eng, fn, reads=(), writes=(), mark=True, waits=()):
        if self.rec is not None:
            self.rec.append(("op", (eng, fn, list(reads), list(writes), mark, list(waits))))
            return None
        self._waits(eng, self._deps(reads, writes, waits))
        if not mark:
            self.q[eng].append(lambda e, fn=fn: fn(e))
            self.pend[eng].extend(reads)
            return None
        if self.ecount[eng] >= 6000:
            self.epoch[eng] += 1
            self.ecount[eng] = 0
        self.ecount[eng] += 1
        s = "c_%s_%d" % (eng, self.epoch[eng])
        h = self.sem(s)
        self.cnt[s] += 1
        tok = (s, self.cnt[s])
        self.q[eng].append(lambda e, fn=fn, h=h: fn(e).then_inc(h, 1))
        self._finish(eng, tok, reads, writes)
        return tok

    def dma(self, eng, fn, semname, reads=(), writes=(), waits=()):
        if self.rec is not None:
            self.rec.append(("dma", (eng, fn, semname, list(reads), list(writes), list(waits))))
            return None
        self._waits(eng, self._deps(reads, writes, waits))
        h = self.sem(semname)
        self.cnt[semname] += 16
        tok = (semname, self.cnt[semname])
        self.q[eng].append(lambda e, fn=fn, h=h: fn(e).then_inc(h, 16))
        for b in reads:
            b.r[semname] = tok
        for b in writes:
            b.w = tok
            b.r = {}
        return tok

    def wait(self, eng, waits):
        self._waits(eng, waits)

    def emit(self):
        nc = self.nc
        with nc.Block() as block:
            @block.tensor
            def _(e):
                for f in self.q["pe"]:
                    f(e)

            @block.scalar
            def _(e):
                for f in self.q["act"]:
                    f(e)

            @block.vector
            def _(e):
                for f in self.q["dve"]:
                    f(e)

            @block.gpsimd
            def _(e):
                for f in self.q["pool"]:
                    f(e)

            @block.sync
            def _(e):
                for f in self.q["sp"]:
                    f(e)


def ACT(P, out, in_, func, reads, writes, bias=None, scale=None, accum_out=None):
    kw = {}
    if bias is not None:
        kw["bias"] = bias
    if scale is not None:
        kw["scale"] = scale
    if accum_out is not None:
        kw["accum_out"] = accum_out
    return P.op("act", lambda e: e.activation(out=out, in_=in_, func=func, **kw), reads, writes)


def TT(P, out, in0, in1, op, reads, writes, eng="dve"):
    return P.op(eng, lambda e: e.tensor_tensor(out=out, in0=in0, in1=in1, op=op), reads, writes)


def TS(P, out, in0, s1, op0, reads, writes, s2=None, op1=None, eng="dve"):
    if op1 is None:
        return P.op(eng, lambda e: e.tensor_scalar(out=out, in0=in0, scalar1=s1, scalar2=None, op0=op0), reads, writes)
    return P.op(eng, lambda e: e.tensor_scalar(out=out, in0=in0, scalar1=s1, scalar2=s2, op0=op0, op1=op1), reads, writes)


def STT(P, out, in0, scalar, in1, op0, op1, reads, writes, accum_out=None):
    if accum_out is None:
        return P.op("dve", lambda e: e.scalar_tensor_tensor(out=out, in0=in0, scalar=scalar, in1=in1, op0=op0, op1=op1), reads, writes)
    return P.op("dve", lambda e: e.scalar_tensor_tensor(out=out, in0=in0, scalar=scalar, in1=in1, op0=op0, op1=op1, accum_out=accum_out), reads, writes)


def COPY(P, out, in_, reads, writes, eng="dve"):
    if eng == "act":
        return P.op("act", lambda e: e.copy(out=out, in_=in_), reads, writes)
    return P.op(eng, lambda e: e.tensor_copy(out=out, in_=in_), reads, writes)


def MM(P, out, lhsT, rhs, start, stop, reads, writes, mark):
    return P.op("pe", lambda e: e.matmul(out, lhsT, rhs, start=start, stop=stop), reads, writes, mark=mark)


def build(nseq=2, dbg=False):
    nc = bass.Bass("TRN2", target_bir_lowering=False)

    def din(name, shape, dt=F32):
        return nc.dram_tensor(name, list(shape), dt, kind="ExternalInput")

    xT_d = din("xT", [nseq, D, SEQ]).ap()
    x_d = din("x", [nseq, SEQ, D]).ap()
    w_in_d = din("w_in", [D, 2560]).ap()
    bvec_d = din("bvec", [128, 20]).ap()
    bv_h = din("b_v", [1, 512])
    convw_d = din("convw", [128, 4 * 31]).ap()
    cvec_d = din("cvec", [128, 12]).ap()
    w_out_d = din("w_out", [D, D]).ap()
    rows_h = din("rows", [5, D])
    wq_d = din("wq", [D, 2048]).ap()
    keysT_d = din("keysT", [128, 2048]).ap()
    pu_d = din("peer_u", [16384, D]).ap()
    pv_d = din("peer_v", [16384, D]).ap()
    etab_d = din("etab", [8, 128, 2048], BF16).ap()
    ident_d = din("ident", [128, 128]).ap()
    iota_d = din("iota", [128, 128], I32).ap()
    masks_d = din("masks", [128, 8], I32).ap()
    out_d = nc.dram_tensor("out", [nseq, SEQ, D], F32, kind="ExternalOutput").ap()
    uvb_d = nc.dram_tensor("uvb", [16384, 2048], BF16, kind="Internal").ap()
    wqb_d = nc.dram_tensor("wqb", [4, 128, 8 * 512], BF16, kind="Internal").ap()
    winb_d = nc.dram_tensor("winb", [D, 2560], BF16, kind="Internal").ap()
    dbg_d = {}
    if dbg:
        dbg_d["mixT"] = nc.dram_tensor("dbg_mixT", [128, 8 * SEQ], F32, kind="ExternalOutput").ap()
        dbg_d["h1"] = nc.dram_tensor("dbg_h1", [128, D], F32, kind="ExternalOutput").ap()
        dbg_d["eid"] = nc.dram_tensor("dbg_eid", [128, 128], I32, kind="ExternalOutput").ap()
        dbg_d["gate"] = nc.dram_tensor("dbg_gate", [128, 128], F32, kind="ExternalOutput").ap()
        dbg_d["acc"] = nc.dram_tensor("dbg_acc", [128, D], F32, kind="ExternalOutput").ap()

    with ExitStack() as st:
        P = Prog(nc, st)

        def sb(name, shape, dt):
            return st.enter_context(nc.sbuf_tensor("s_" + name, list(shape), dt))

        lnp = sb("lnp", [128, 5 * D], F32)
        keysT = sb("keysT", [128, 2048], BF16)
        identb = sb("identb", [128, 128], BF16)
        bvec = sb("bvec", [128, 20], F32)
        cvec = sb("cvec", [128, 12], F32)
        convw = sb("convw", [128, 4 * 31], F32)
        iota = sb("iota", [128, 128], I32)
        masks = sb("masks", [128, 8], I32)
        onesm = sb("onesm", [128, 128], F32)
        bvb = sb("bvb", [128, 512], F32)
        identf = sb("identf", [128, 128], F32)
        w_out = sb("w_out", [128, 8 * D], BF16)
        mixT = sb("mixT", [128, 8 * SEQ], BF16)
        R1 = sb("R1", [128, 12288], F32)
        WK = sb("WK", [128, 21248], F32)
        banks = [st.enter_context(nc.psum_tensor("bank%d" % i, [128, 512], F32)) for i in range(8)]
        bankB = [Buf("bank%d" % i) for i in range(8)]

        def carve(region, off_bytes, nbytes, dt):
            assert off_bytes % 4 == 0 and nbytes % 4 == 0
            a = region[:, off_bytes // 4:(off_bytes + nbytes) // 4]
            return a if dt == F32 else a.bitcast(dt)

        cB = Buf("consts")
        P.dma("sp", lambda e: e.dma_start(out=bvec[:, :], in_=bvec_d), "d_c", writes=[cB])
        P.dma("sp", lambda e: e.dma_start(out=cvec[:, :], in_=cvec_d), "d_c", writes=[cB])
        P.dma("sp", lambda e: e.dma_start(out=convw[:, :], in_=convw_d), "d_c", writes=[cB])
        P.dma("sp", lambda e: e.dma_start(out=iota[:, :], in_=iota_d), "d_c", writes=[cB])
        P.dma("sp", lambda e: e.dma_start(out=masks[:, :], in_=masks_d), "d_c", writes=[cB])
        P.dma("sp", lambda e: e.dma_start(out=identf[:, :], in_=ident_d), "d_c", writes=[cB])
        P.dma("sp", lambda e: e.dma_start(out=bvb[:, :], in_=bass.AP(bv_h, 0, [[0, 128], [1, 512]])), "d_c", writes=[cB])
        for i in range(5):
            P.dma("sp", lambda e, i=i: e.dma_start(out=lnp[:, i * D:(i + 1) * D],
                                                    in_=bass.AP(rows_h, i * D, [[0, 128], [1, D]])), "d_c", writes=[cB])
        P.dma("pool", lambda e: e.dma_start(out=identb[:, :], in_=ident_d), "d_cp", writes=[cB])
        P.dma("pool", lambda e: e.dma_start(out=keysT[:, :], in_=keysT_d), "d_cp", writes=[cB])
        woB = Buf("w_out")
        w_out_v = w_out[:, :].rearrange("p (c n) -> p c n", c=8)
        w_out_dv = w_out_d.rearrange("(c p) n -> p c n", p=128)
        for c in range(8):
            P.dma("pool", lambda e, c=c: e.dma_start(out=w_out_v[:, c, :], in_=w_out_dv[:, c, :]), "d_cp", writes=[woB])
        onesB = Buf("onesm")
        P.op("dve", lambda e: e.memset(onesm[:, :], 1.0 / 512.0), writes=[onesB])

        b_out_bc = lnp[:, 0:D]
        g1_bc, b1_bc = lnp[:, D:2 * D], lnp[:, 2 * D:3 * D]
        g2_bc, b2_bc = lnp[:, 3 * D:4 * D], lnp[:, 4 * D:5 * D]

        mixT_v = mixT[:, :].rearrange("p (c t) -> p c t", c=8)
        mixB = [Buf("mix%d" % i) for i in range(NT)]

        uvB = Buf("uvb")
        wqbB = Buf("wqb")
        wqb_v = wqb_d.rearrange("q p (c n) -> q p c n", c=8)
        winbB = Buf("winb")
        winb_dv = winb_d.rearrange("(c p) n -> p c n", p=128)

        def convert_small():
            for p_ in range(4):
                for c in range(8):
                    P.dma("pool", lambda e, p_=p_, c=c: e.dma_start(out=wqb_v[p_, :, c, :],
                                                                   in_=wq_d[c * 128:(c + 1) * 128, p_ * 512:(p_ + 1) * 512]),
                          "d_cv", writes=[wqbB])
            for c in range(8):
                for hf in range(2):
                    P.dma("pool", lambda e, c=c, hf=hf: e.dma_start(out=winb_d[c * 128:(c + 1) * 128, hf * 1280:(hf + 1) * 1280],
                                                                   in_=w_in_d[c * 128:(c + 1) * 128, hf * 1280:(hf + 1) * 1280]),
                          "d_cv", writes=[winbB])

        def convert_chunk(tab_d, col0, ch):
            r0 = ch * 2048
            P.dma("pool", lambda e: e.dma_start(out=uvb_d[r0:r0 + 2048, col0:col0 + 1024], in_=tab_d[r0:r0 + 2048, :]),
                  "d_cv", writes=[uvB])

        xT_dv = xT_d.rearrange("s (c p) t -> s p c t", p=128)
        w_in_dv = w_in_d.rearrange("(c p) n -> p c n", p=128)
        wq_dv = wq_d.rearrange("(c p) n -> p c n", p=128)

        def conv_pass(s):
            wic = carve(R1, 0, 16384, BF16).rearrange("p (c n) -> p c n", c=8)
            diag = carve(R1, 16384, 31744, BF16).rearrange("p (g n) -> p g n", n=128)
            wicB, diagB = Buf("wic"), Buf("diag")
            xts = [carve(WK, i * 8192, 8192, BF16).rearrange("p (c t) -> p c t", c=8) for i in range(2)]
            xtB = [Buf("xt0"), Buf("xt1")]
            hdn = carve(WK, 16384, 16640, BF16).rearrange("p (c t) -> p c t", c=4)
            hdnB = [Buf("hdn%d" % i) for i in range(4)]
            hpadB = Buf("hpad")
            o = 16384 + 16640
            z = carve(WK, o, 8192, F32).rearrange("p (c t) -> p c t", c=4); o += 8192
            zsq = carve(WK, o, 8192, F32).rearrange("p (c t) -> p c t", c=4); o += 8192
            sig = [carve(WK, o + i * 2048, 2048, F32) for i in range(2)]; o += 4096
            mean_sb = carve(WK, o, 2048, F32); o += 2048
            msq = carve(WK, o, 2048, F32); o += 2048
            rstd = carve(WK, o, 2048, F32); o += 2048
            t1 = [carve(WK, o + i * 2048, 2048, F32) for i in range(2)]; o += 4096
            stage = carve(WK, 64000, 8192, F32).rearrange("p (c t) -> p c t", c=8)
            stgB = Buf("stage")
            assert o <= 64000
            zB, zsqB, meanB, msqB, rstdB = Buf("z"), Buf("zsq"), Buf("mean"), Buf("msq"), Buf("rstd")
            sigB = [Buf("sig0"), Buf("sig1")]
            t1B = [Buf("t1a"), Buf("t1b")]

            for c in range(8):
                if s == 0:
                    P.dma("pool", lambda e, c=c: e.dma_start(out=wic[:, c, :], in_=w_in_dv[:, c, 0:1024]), "d_w", writes=[wicB])
                else:
                    P.dma("sp", lambda e, c=c: e.dma_start(out=wic[:, c, :], in_=winb_dv[:, c, 0:1024]), "d_w2", reads=[winbB], writes=[wicB])
            for g in range(4 * 31):
                P.op("dve", lambda e, g=g: e.tensor_scalar(out=diag[:, g, :], in0=identf[:, :], scalar1=convw[:, g:g + 1],
                                                           scalar2=None, op0=ALU.mult), reads=[cB], writes=[diagB])
            P.op("dve", lambda e: e.memset(hdn[:, :, 0:32], 0.0), writes=[hpadB])

            for tb in range(4):
                xb, xB = xts[tb % 2], xtB[tb % 2]
                for hf_ in range(2):
                    P.dma("sp", lambda e, tb=tb, hf_=hf_: e.dma_start(
                        out=stage, in_=xT_dv[s, :, :, tb * 512 + hf_ * 256:tb * 512 + (hf_ + 1) * 256]), "d_xs", writes=[stgB])
                    COPY(P, xb[:, :, hf_ * 256:(hf_ + 1) * 256], stage, [stgB], [xB], eng="act")
                if s == 0:
                    convert_chunk(pu_d, 0, 2 * tb)
                    convert_chunk(pu_d, 0, 2 * tb + 1)
                for cc in range(4):
                    pa, pg = banks[(cc % 2) * 2], banks[(cc % 2) * 2 + 1]
                    paB, pgB = bankB[(cc % 2) * 2], bankB[(cc % 2) * 2 + 1]
                    for c in range(8):
                        MM(P, pa[:, :], wic[:, c, cc * 128:(cc + 1) * 128], xb[:, c, :], c == 0, c == 7,
                           [wicB, xB], [paB], c == 7)
                    for c in range(8):
                        MM(P, pg[:, :], wic[:, c, 512 + cc * 128:512 + (cc + 1) * 128], xb[:, c, :], c == 0, c == 7,
                           [wicB, xB], [pgB], c == 7)
                    sg, sgB = sig[cc % 2], sigB[cc % 2]
                    ACT(P, sg, pg[:, :], AF.Sigmoid, [pgB, cB], [sgB], bias=bvec[:, 4 + cc:5 + cc])
                    STT(P, hdn[:, cc, 32 + tb * 512:32 + (tb + 1) * 512], pa[:, :], bvec[:, cc:cc + 1], sg,
                        ALU.add, ALU.mult, [paB, sgB, cB], [hdnB[tb]])
                for cc in range(4):
                    pc, pcB = banks[4 + cc % 2], bankB[4 + cc % 2]
                    rd = [diagB, hdnB[tb], hpadB] + ([hdnB[tb - 1]] if tb > 0 else [])
                    for k in range(31):
                        MM(P, pc[:, :], diag[:, cc * 31 + k, :], hdn[:, cc, 2 + tb * 512 + k:2 + tb * 512 + k + 512],
                           k == 0, k == 30, rd, [pcB], k == 30)
                    ACT(P, z[:, cc, :], pc[:, :], AF.Identity, [pcB, cB], [zB], bias=cvec[:, cc:cc + 1])
                    ACT(P, zsq[:, cc, :], pc[:, :], AF.Square, [pcB, cB], [zsqB], bias=cvec[:, cc:cc + 1])
                pm, pmB, pq, pqB = banks[6], bankB[6], banks[7], bankB[7]
                for cc in range(4):
                    MM(P, pm[:, :], onesm[:, :], z[:, cc, :], cc == 0, cc == 3, [onesB, zB], [pmB], cc == 3)
                for cc in range(4):
                    MM(P, pq[:, :], onesm[:, :], zsq[:, cc, :], cc == 0, cc == 3, [onesB, zsqB], [pqB], cc == 3)
                COPY(P, mean_sb, pm[:, :], [pmB], [meanB], eng="act")
                ACT(P, msq, pm[:, :], AF.Square, [pmB], [msqB])
                TT(P, rstd, pq[:, :], msq, ALU.subtract, [pqB, msqB], [rstdB])
                ACT(P, rstd, rstd, AF.Sqrt, [rstdB], [rstdB], bias=LN_EPS)
                P.op("dve", lambda e: e.reciprocal(out=rstd, in_=rstd), [rstdB], [rstdB])
                for cc in range(4):
                    tt_, ttB = t1[cc % 2], t1B[cc % 2]
                    TT(P, tt_, z[:, cc, :], mean_sb, ALU.subtract, [zB, meanB], [ttB])
                    TT(P, tt_, tt_, rstd, ALU.mult, [ttB, rstdB], [ttB])
                    ACT(P, mixT_v[:, cc, tb * 512:(tb + 1) * 512], tt_, AF.Silu, [ttB, cB], mixB[4 * tb:4 * tb + 4],
                        scale=cvec[:, 4 + cc:5 + cc], bias=cvec[:, 8 + cc:9 + cc])

        def attn_pass(s):
            wia = carve(R1, 0, 24576, BF16).rearrange("p (c n) -> p c n", c=8)
            wiaB = Buf("wia")
            Es = [carve(R1, 24576 + i * 4096, 4096, BF16) for i in range(2)]
            EB = [Buf("E0"), Buf("E1")]
            atm = carve(R1, 32768, 16384, BF16).rearrange("p (i f) -> p i f", i=16)
            atmB = [Buf("atm%d" % i) for i in range(16)]
            xts = [carve(WK, i * 8192, 8192, BF16).rearrange("p (c t) -> p c t", c=8) for i in range(2)]
            xtB = [Buf("xt0"), Buf("xt1")]
            qT = carve(WK, 16384, 16384, BF16).rearrange("p (j t) -> p j t", j=4)
            kT = carve(WK, 32768, 16384, BF16).rearrange("p (j t) -> p j t", j=4)
            Vp = carve(WK, 49152, 16896, BF16).rearrange("p (t h c) -> p t h c", t=16, h=8)
            o = 49152 + 16896
            pTs = [carve(WK, o + i * 1024, 1024, BF16) for i in range(4)]; o += 4096
            rl = [carve(WK, o + i * 4, 4, F32) for i in range(2)]; o += 8
            pTB = [Buf("pT%d" % i) for i in range(4)]
            assert o <= 70400
            stage = carve(WK, 70400, 8192, F32).rearrange("p (c t) -> p c t", c=8)
            stgB = Buf("stage")
            rlB = [Buf("rl0"), Buf("rl1")]
            qkB = [Buf("qk%d" % i) for i in range(4)]
            vB = [Buf("v%d" % i) for i in range(4)]

            for c in range(8):
                if s == 0:
                    P.dma("pool", lambda e, c=c: e.dma_start(out=wia[:, c, :], in_=w_in_dv[:, c, 1024:2560]), "d_w", writes=[wiaB])
                else:
                    P.dma("sp", lambda e, c=c: e.dma_start(out=wia[:, c, :], in_=winb_dv[:, c, 1024:2560]), "d_w2", reads=[winbB], writes=[wiaB])
            Vflat = carve(WK, 49152, 16896, BF16)
            P.op("dve", lambda e: e.memset(Vflat, 1.0), writes=vB)

            for tb in range(4):
                xb, xB = xts[tb % 2], xtB[tb % 2]
                for hf_ in range(2):
                    P.dma("sp", lambda e, tb=tb, hf_=hf_: e.dma_start(
                        out=stage, in_=xT_dv[s, :, :, tb * 512 + hf_ * 256:tb * 512 + (hf_ + 1) * 256]), "d_xs", writes=[stgB])
                    COPY(P, xb[:, :, hf_ * 256:(hf_ + 1) * 256], stage, [stgB], [xB], eng="act")
                if s == 0:
                    convert_chunk(pv_d, 1024, 2 * tb)
                    convert_chunk(pv_d, 1024, 2 * tb + 1)
                    if tb == 1:
                        convert_small()
                for j in range(8):
                    pb, pbB = banks[j % 2], bankB[j % 2]
                    for c in range(8):
                        MM(P, pb[:, :], wia[:, c, j * 128:(j + 1) * 128], xb[:, c, :], c == 0, c == 7, [wiaB, xB], [pbB], c == 7)
                    dst = (qT if j < 4 else kT)[:, j % 4, tb * 512:(tb + 1) * 512]
                    if j % 2 == 0:
                        ACT(P, dst, pb[:, :], AF.Identity, [pbB, cB], [qkB[tb]], bias=bvec[:, 8 + j:9 + j])
                    else:
                        TS(P, dst, pb[:, :], bvec[:, 8 + j:9 + j], ALU.add, [pbB, cB], [qkB[tb]])
                for tt in range(4):
                    tile = tb * 4 + tt
                    pb, pbB = banks[2 + tt % 2], bankB[2 + tt % 2]
                    for c in range(8):
                        MM(P, pb[:, :], xb[:, c, tt * 128:(tt + 1) * 128], wia[:, c, 1024:1536], c == 0, c == 7, [wiaB, xB], [pbB], c == 7)
                    TT(P, Vp[:, tile, :, 0:64], pb[:, :].rearrange("p (h c) -> p h c", h=8),
                       bvb[:, :].rearrange("p (h c) -> p h c", h=8), ALU.add, [pbB, cB], [vB[tb]])

            dmax = DMAX_BLK
            groups = [(h, I, g) for h in range(8) for I in range(16) for g in range((min(I, dmax[h]) + 4) // 4)]
            n_g = len(groups)

            def load_E(h):
                Eb, EbB = Es[h % 2], EB[h % 2]
                P.dma("sp", lambda e, h=h, Eb=Eb: e.dma_start(out=Eb, in_=etab_d[h, :, :]), "d_e%d" % (h % 2), writes=[EbB])

            def geo(i):
                h, I, g = groups[i]
                d0 = 4 * g
                nd = min(4, min(I, dmax[h]) + 1 - d0)
                return h, I, g, d0, nd, slice((h % 2) * 64, (h % 2) * 64 + 64), h // 2

            def stageA(i):
                h, I, g, d0, nd, pr, j = geo(i)
                ps_, psB = banks[2 + i % 4], bankB[2 + i % 4]
                for s_ in range(nd):
                    J = I - (d0 + s_)
                    MM(P, ps_[:, s_ * 128:(s_ + 1) * 128], kT[pr, j, J * 128:(J + 1) * 128],
                       qT[pr, j, I * 128:(I + 1) * 128], True, True, [qkB[J // 4], qkB[I // 4]], [psB], s_ == nd - 1)

            def stageB(i):
                h, I, g, d0, nd, pr, j = geo(i)
                ps_, psB = banks[2 + i % 4], bankB[2 + i % 4]
                pt, ptB = pTs[i % 4], pTB[i % 4]
                Eb, EbB = Es[h % 2], EB[h % 2]
                ACT(P, pt[:, 0:nd * 128], ps_[:, 0:nd * 128], AF.Exp, [psB], [ptB], scale=0.125)
                TT(P, pt[:, 0:nd * 128], pt[:, 0:nd * 128], Eb[:, d0 * 128:(d0 + nd) * 128], ALU.mult, [ptB, EbB], [ptB])
                if I == 0 and g == 0 and h + 1 < 8:
                    load_E(h + 1)

            def stageC(i):
                h, I, g, d0, nd, pr, j = geo(i)
                ngr = (min(I, dmax[h]) + 4) // 4
                pi = h * 16 + I
                po, poB = banks[6 + pi % 2], bankB[6 + pi % 2]
                pt, ptB = pTs[i % 4], pTB[i % 4]
                for s_ in range(nd):
                    J = I - (d0 + s_)
                    first = (g == 0 and s_ == 0)
                    last = (g == ngr - 1 and s_ == nd - 1)
                    MM(P, po[:, 0:65], pt[:, s_ * 128:(s_ + 1) * 128], Vp[:, J, h, 0:65], first, last,
                       [ptB, vB[J // 4]], [poB], last or s_ == nd - 1)
                if g == ngr - 1:
                    r_, rB = rl[pi % 2], rlB[pi % 2]
                    P.op("dve", lambda e, r_=r_, po=po: e.reciprocal(out=r_, in_=po[:, 64:65]), [poB], [rB])
                    TS(P, atm[:, I, h * 64:(h + 1) * 64], po[:, 0:64], r_, ALU.mult, [poB, rB], [atmB[I]])

            load_E(0)
            stageA(0)
            stageA(1)
            for i in range(n_g):
                if i + 2 < n_g:
                    stageA(i + 2)
                stageB(i)
                stageC(i)
            for I in range(16):
                pb, pbB = banks[I % 2], bankB[I % 2]
                pbb = pb[:, :].bitcast(BF16)
                for fc in range(4):
                    P.op("pe", lambda e, pbb=pbb, I=I, fc=fc: e.transpose(pbb[:, fc * 128:(fc + 1) * 128], atm[:, I, fc * 128:(fc + 1) * 128], identb[:, :]),
                         [atmB[I], cB], [pbB], mark=(fc == 3))
                COPY(P, mixT_v[:, 4:8, I * 128:(I + 1) * 128], pbb[:, 0:512].rearrange("p (c t) -> p c t", c=4),
                     [pbB], [mixB[I]], eng=("act" if I % 2 == 0 else "dve"))

        def layer_norm(src, srcB, dst, dstB, g_bc, b_bc, stt, mv, sd, stB):
            P.op("dve", lambda e: e.bn_stats(out=stt[:, 0:6], in_=src[:, 0:512]), [srcB], [stB])
            P.op("dve", lambda e: e.bn_stats(out=stt[:, 6:12], in_=src[:, 512:1024]), [srcB], [stB])
            P.op("dve", lambda e: e.bn_aggr(out=mv[:, 0:2], in_=stt[:, 0:12]), [stB], [stB])
            ACT(P, sd, mv[:, 1:2], AF.Sqrt, [stB], [stB], bias=LN_EPS)
            P.op("dve", lambda e: e.reciprocal(out=sd, in_=sd), [stB], [stB])
            TS(P, dst, src, mv[:, 0:1], ALU.subtract, [srcB, stB], [dstB], s2=sd, op1=ALU.mult)
            TT(P, dst, dst, g_bc, ALU.mult, [dstB, cB], [dstB])
            TT(P, dst, dst, b_bc, ALU.add, [dstB, cB], [dstB])

        def peer_phase(s):
            wqr = carve(R1, 0, 8192, BF16).rearrange("p (c n) -> p c n", c=8)
            wqrB = Buf("wqr")

            class St:
                pass
            states = []
            for par in range(2):
                S = St()
                base = 8192 + par * 8192
                S.h1 = carve(R1, base, 4096, F32); S.h1b = carve(R1, base + 4096, 2048, BF16)
                S.eid = carve(R1, base + 6144, 512, I32); S.gate = carve(R1, base + 6656, 512, F32)
                S.dots = carve(R1, base + 7168, 512, F32); S.w = carve(R1, base + 7680, 512, F32)
                S.h1B, S.h1bB, S.eidB, S.gateB, S.dotsB, S.wB = (Buf("h1_%d" % par), Buf("h1b_%d" % par), Buf("eid_%d" % par),
                                                                  Buf("gate_%d" % par), Buf("dots_%d" % par), Buf("w_%d" % par))
                S.acc = (banks[4 + 2 * par], banks[5 + 2 * par])
                S.accB = (bankB[4 + 2 * par], bankB[5 + 2 * par])
                states.append(S)
            o = 0

            def wk(nbytes, dt):
                nonlocal o
                a_ = carve(WK, o, nbytes, dt)
                o += nbytes
                return a_
            XR = wk(8192, F32)
            xt_, r_ = XR[:, 0:1024], XR[:, 1024:2048]
            C = XR
            h1T = wk(2048, BF16).rearrange("p (c t) -> p c t", c=8)
            qTs = wk(4096, BF16).rearrange("p (g t) -> p g t", g=16)
            A = wk(8192, F32); Bq = wk(8192, F32)
            tmp = wk(1024, F32); v1 = wk(1024, F32); i1 = wk(1024, I32); i1f = wk(1024, F32); i1s = wk(512, F32)
            wv = wk(512, F32); bv = wk(512, F32); ex = wk(512, F32); actv = wk(512, F32)
            se = wk(64, F32); stt = wk(48, F32); mv = wk(8, F32); sd = wk(8, F32)
            scr = wk(2048, BF16)
            FB = wk(4096, F32)
            stt2 = wk(48, F32); mv2 = wk(8, F32); sd2 = wk(8, F32)
            dg = [wk(256, BF16) for _ in range(8)]
            prod = [wk(2048, BF16) for _ in range(2)]
            prodB = [Buf("prod0"), Buf("prod1")]
            o = (o + 63) // 64 * 64
            nwk = (21248 * 4 - o) // 4096
            G = [wk(4096, BF16) for _ in range(nwk)]
            assert o <= 21248 * 4, o
            G += [carve(R1, 24576 + i * 4096, 4096, BF16) for i in range(NG - nwk)]
            assert 24576 + (NG - nwk) * 4096 <= 49152
            Bn = {n: Buf(n) for n in ["XR", "h1T", "qTs", "A", "Bq", "tmp", "v1", "i1", "i1f", "i1s", "wv", "bv", "ex", "se", "actv", "st", "FB", "st2"]}
            GB = [Buf("G%d" % i) for i in range(NG)]
            dgB = [Buf("dg%d" % i) for i in range(8)]
            keysT_v = keysT[:, :].rearrange("p (g n) -> p g n", g=16)
            cnt = {"gi": 0, "dg": 0, "pr": 0}

            def prep(tt):
                P.rec = []

                def idle(n):
                    P.rec.extend([None] * n)
                S = states[tt % 2]
                tsl = slice(tt * 128, (tt + 1) * 128)
                first_dbg = dbg and s == 0 and tt == 0
                P.dma("sp", lambda e: e.dma_start(out=xt_, in_=x_d[s, tsl, :]), "d_x", writes=[Bn["XR"]])
                py = [banks[0], banks[1]]
                for hf in range(2):
                    for fc in range(8):
                        MM(P, py[hf][:, :], mixT_v[:, fc, tsl], w_out_v[:, fc, hf * 512:(hf + 1) * 512], fc == 0, fc == 7,
                           [mixB[tt], woB], [bankB[hf]], fc == 7)
                idle(20)
                for hf in range(2):
                    cs = slice(hf * 512, (hf + 1) * 512)
                    STT(P, r_[:, cs], xt_[:, cs], ALPHA, py[hf][:, :], ALU.mult, ALU.add, [Bn["XR"], bankB[hf]], [Bn["XR"]])
                TT(P, r_, r_, b_out_bc, ALU.add, [Bn["XR"], cB], [Bn["XR"]])
                P.op("dve", lambda e: e.bn_stats(out=stt[:, 0:6], in_=r_[:, 0:512]), [Bn["XR"]], [Bn["st"]])
                P.op("dve", lambda e: e.bn_stats(out=stt[:, 6:12], in_=r_[:, 512:1024]), [Bn["XR"]], [Bn["st"]])
                P.op("dve", lambda e: e.bn_aggr(out=mv[:, 0:2], in_=stt[:, 0:12]), [Bn["st"]], [Bn["st"]])
                idle(4)
                ACT(P, sd[:, 0:1], mv[:, 1:2], AF.Sqrt, [Bn["st"]], [Bn["st"]], bias=LN_EPS)
                idle(4)
                P.op("dve", lambda e: e.reciprocal(out=sd[:, 0:1], in_=sd[:, 0:1]), [Bn["st"]], [Bn["st"]])
                TS(P, S.h1, r_, mv[:, 0:1], ALU.subtract, [Bn["XR"], Bn["st"]], [S.h1B], s2=sd[:, 0:1], op1=ALU.mult)
                TT(P, S.h1, S.h1, g1_bc, ALU.mult, [S.h1B, cB], [S.h1B])
                TT(P, S.h1, S.h1, b1_bc, ALU.add, [S.h1B, cB], [S.h1B])
                if first_dbg:
                    P.dma("sp", lambda e: e.dma_start(out=dbg_d["h1"], in_=S.h1), "d_dbg", reads=[S.h1B])
                idle(4)
                COPY(P, S.h1b, S.h1, [S.h1B], [S.h1bB], eng="act")
                idle(4)
                pt_ = banks[3][:, :].bitcast(BF16)
                for c in range(8):
                    P.op("pe", lambda e, c=c: e.transpose(pt_[:, c * 128:(c + 1) * 128], S.h1b[:, c * 128:(c + 1) * 128], identb[:, :]),
                         [S.h1bB, cB], [bankB[3]], mark=(c == 7))
                idle(4)
                COPY(P, h1T, pt_[:, :].rearrange("p (c t) -> p c t", c=8), [bankB[3]], [Bn["h1T"]], eng="act")
                idle(4)
                for p_ in range(4):
                    bk = 2 + p_ % 2
                    P.dma("sp", lambda e, p_=p_: e.dma_start(out=wqr, in_=wqb_v[p_]), "d_wq", reads=[wqbB], writes=[wqrB])
                    for gq in range(4 * p_, 4 * p_ + 4):
                        for c in range(8):
                            MM(P, banks[bk][:, (gq % 4) * 128:(gq % 4 + 1) * 128], wqr[:, c, (gq % 4) * 128:(gq % 4 + 1) * 128], h1T[:, c, :],
                               c == 0, c == 7, [wqrB, Bn["h1T"]], [bankB[bk]], (c == 7 and gq % 4 == 3))
                    idle(3)
                    COPY(P, qTs[:, 4 * p_:4 * p_ + 4, :], banks[bk][:, :].rearrange("p (g t) -> p g t", g=4), [bankB[bk]], [Bn["qTs"]], eng="act")
                idle(4)
                for p_ in range(4):
                    bk = 2 + p_ % 2
                    for gq in range(4 * p_, 4 * p_ + 4):
                        MM(P, banks[bk][:, (gq % 4) * 128:(gq % 4 + 1) * 128], qTs[:, gq, :], keysT_v[:, gq, :], True, True,
                           [Bn["qTs"], cB], [bankB[bk]], gq % 4 == 3)
                    idle(3)
                    COPY(P, A[:, p_ * 512:(p_ + 1) * 512], banks[bk][:, :], [bankB[bk]], [Bn["A"]], eng="act")
                idle(4)
                STT(P, Bq.bitcast(I32).rearrange("p (g n) -> p g n", g=16), A.bitcast(I32).rearrange("p (g n) -> p g n", g=16),
                    masks[:, 0:1], iota[:, :].unsqueeze(1).to_broadcast([128, 16, 128]), ALU.bitwise_and, ALU.bitwise_or,
                    [Bn["A"], cB], [Bn["Bq"]])
                for gq in range(16):
                    src = Bq[:, gq * 128:(gq + 1) * 128]
                    P.op("dve", lambda e, gq=gq, src=src: e.max(out=v1[:, gq * 16:gq * 16 + 8], in_=src), [Bn["Bq"]], [Bn["v1"]])
                    P.op("dve", lambda e, gq=gq, src=src: e.match_replace(out=tmp[:, 0:128], in_to_replace=v1[:, gq * 16:gq * 16 + 8],
                                                                         in_values=src, imm_value=-1e30), [Bn["Bq"], Bn["v1"]], [Bn["tmp"]])
                    P.op("dve", lambda e, gq=gq: e.max(out=v1[:, gq * 16 + 8:gq * 16 + 16], in_=tmp[:, 0:128]), [Bn["tmp"]], [Bn["v1"]])
                TS(P, i1, v1.bitcast(I32), masks[:, 1:2], ALU.bitwise_and, [Bn["v1"], cB], [Bn["i1"]])
                COPY(P, i1f, i1, [Bn["i1"]], [Bn["i1f"]])
                v1v = v1.rearrange("p (h q a) -> p h q a", h=8, q=2)
                i1fv = i1f.rearrange("p (h q a) -> p h q a", h=8, q=2)
                TS(P, i1s.rearrange("p (h a) -> p h a", h=8), i1fv[:, :, 0, :], 128.0, ALU.mult, [Bn["i1f"]], [Bn["i1s"]])
                candv = A.rearrange("p (h a b) -> p h a b", h=8, a=16)
                TT(P, candv, v1v[:, :, 0, :].unsqueeze(3).to_broadcast([128, 8, 16, 16]),
                   v1v[:, :, 1, :].unsqueeze(2).to_broadcast([128, 8, 16, 16]), ALU.add, [Bn["v1"], Bn["A"]], [Bn["A"]])
                TT(P, C.rearrange("p (h a b) -> p h a b", h=8, a=16),
                   i1s.rearrange("p (h a) -> p h a", h=8).unsqueeze(3).to_broadcast([128, 8, 16, 16]),
                   i1fv[:, :, 1, :].unsqueeze(2).to_broadcast([128, 8, 16, 16]), ALU.add, [Bn["i1s"], Bn["i1f"]], [Bn["XR"]])
                COPY(P, C.bitcast(I32), C, [Bn["XR"]], [Bn["XR"]])
                STT(P, Bq.bitcast(I32), A.bitcast(I32), masks[:, 2:3], C.bitcast(I32), ALU.bitwise_and, ALU.bitwise_or,
                    [Bn["A"], Bn["XR"], cB], [Bn["Bq"]])
                for hh in range(8):
                    for (srcbuf, srcn, dstbuf, dstn) in ((Bq, "Bq", wv, "wv"), (A, "A", bv, "bv")):
                        src = srcbuf[:, hh * 256:(hh + 1) * 256]
                        P.op("dve", lambda e, hh=hh, src=src, dstbuf=dstbuf: e.max(out=dstbuf[:, hh * 16:hh * 16 + 8], in_=src),
                             [Bn[srcn]], [Bn[dstn]])
                        P.op("dve", lambda e, hh=hh, src=src, dstbuf=dstbuf: e.match_replace(
                            out=tmp[:, 0:256], in_to_replace=dstbuf[:, hh * 16:hh * 16 + 8], in_values=src, imm_value=-1e30),
                            [Bn[srcn], Bn[dstn]], [Bn["tmp"]])
                        P.op("dve", lambda e, hh=hh, dstbuf=dstbuf: e.max(out=dstbuf[:, hh * 16 + 8:hh * 16 + 16], in_=tmp[:, 0:256]),
                             [Bn["tmp"]], [Bn[dstn]])
                TS(P, S.eid, wv.bitcast(I32), masks[:, 3:4], ALU.bitwise_and, [Bn["wv"], cB], [S.eidB])
                bvv = bv.rearrange("p (h k) -> p h k", h=8)
                TT(P, ex.rearrange("p (h k) -> p h k", h=8), bvv, bvv[:, :, 0:1].to_broadcast([128, 8, 16]), ALU.subtract,
                   [Bn["bv"]], [Bn["ex"]])
                idle(3)
                ACT(P, ex, ex, AF.Exp, [Bn["ex"]], [Bn["ex"]])
                idle(3)
                P.op("dve", lambda e: e.tensor_reduce(out=se[:, 0:8], in_=ex.rearrange("p (h k) -> p h k", h=8), axis=AX.X, op=ALU.add),
                     [Bn["ex"]], [Bn["se"]])
                P.op("dve", lambda e: e.reciprocal(out=se[:, 0:8], in_=se[:, 0:8]), [Bn["se"]], [Bn["se"]])
                TT(P, S.gate.rearrange("p (h k) -> p h k", h=8), ex.rearrange("p (h k) -> p h k", h=8),
                   se[:, 0:8].unsqueeze(2).to_broadcast([128, 8, 16]), ALU.mult, [Bn["ex"], Bn["se"]], [S.gateB])
                if first_dbg:
                    P.dma("sp", lambda e: e.dma_start(out=dbg_d["eid"], in_=S.eid), "d_dbg", reads=[S.eidB])
                    P.dma("sp", lambda e: e.dma_start(out=dbg_d["gate"], in_=S.gate), "d_dbg", reads=[S.gateB])
                rec, P.rec = P.rec, None
                return rec

            def half_a(S, g, dot_tok):
                g4 = slice(g * 4, g * 4 + 4)
                t_g = P.op("act", lambda e: e.activation(out=actv[:, g4], in_=S.dots[:, g4], func=AF.Gelu), [], [Bn["actv"]], waits=dot_tok)
                S.dotsB.r["act"] = t_g

            def half_b(S, g, slotbuf):
                g4 = slice(g * 4, g * 4 + 4)
                TT(P, S.w[:, g4], actv[:, g4], S.gate[:, g4], ALU.mult, [Bn["actv"], S.gateB], [S.wB])
                for j in range(4):
                    sl = g * 4 + j
                    b = slotbuf[sl]
                    k = cnt["dg"] % 8
                    cnt["dg"] += 1
                    ACT(P, dg[k], identf[:, :], AF.Copy, [S.wB, cB], [dgB[k]], scale=S.w[:, sl:sl + 1])
                    for hf in range(2):
                        MM(P, S.acc[hf][:, :], dg[k], G[b][:, 1024 + hf * 512:1024 + (hf + 1) * 512], sl == 0, sl == 127,
                           [dgB[k], GB[b]], [S.accB[hf]], hf == 1 or sl == 127)

            def consume(tt, nxt):
                S = states[tt % 2]
                slotbuf = {}
                pending = None
                for g in range(32):
                    if pending is not None:
                        half_a(S, pending[0], pending[1])
                    toks = []
                    for j in range(4):
                        sl = g * 4 + j
                        b = cnt["gi"] % NG
                        cnt["gi"] += 1
                        slotbuf[sl] = b
                        P.dma("pool", lambda e, b=b, sl=sl: e.indirect_dma_start(
                            out=G[b], out_offset=None, in_=uvb_d,
                            in_offset=bass.IndirectOffsetOnAxis(ap=S.eid[:, sl:sl + 1].bitcast(U32), axis=0)),
                            "d_g%d" % b, reads=[S.eidB, uvB], writes=[GB[b]])
                        if j % 2 == 1:
                            tok = STT(P, scr, G[b][:, 0:1024], 1.0, S.h1b, ALU.mult, ALU.mult, [GB[b], S.h1bB],
                                      ([S.dotsB] if sl == 0 else []), accum_out=S.dots[:, sl:sl + 1])
                        else:
                            kp = cnt["pr"] % 2
                            cnt["pr"] += 1
                            TT(P, prod[kp], G[b][:, 0:1024], S.h1b, ALU.mult, [GB[b], S.h1bB], [prodB[kp]])
                            tok = P.op("act", lambda e, kp=kp, sl=sl: e.activation(out=prod[kp], in_=prod[kp], func=AF.Copy,
                                                                                   accum_out=S.dots[:, sl:sl + 1]),
                                       [prodB[kp]], [prodB[kp]] + ([S.dotsB] if sl == 0 else []))
                        toks.append(tok)
                        if j == 1 and pending is not None:
                            half_b(S, pending[0], slotbuf)
                        n_tot = n_dve = 0
                        while nxt and n_tot < 5 and n_dve < 2:
                            it = nxt.pop(0)
                            n_tot += 1
                            if it is not None and it[1][0] == "dve":
                                n_dve += 1
                            P.replay(it)
                    pending = (g, toks)
                half_a(S, pending[0], pending[1])
                half_b(S, pending[0], slotbuf)
                while nxt:
                    P.replay(nxt.pop(0))

            def finalize(tt):
                S = states[tt % 2]
                tsl = slice(tt * 128, (tt + 1) * 128)
                if dbg and s == 0 and tt == 0:
                    for hf in range(2):
                        COPY(P, FB[:, hf * 512:(hf + 1) * 512], S.acc[hf][:, :], [S.accB[hf]], [Bn["FB"]])
                    P.dma("sp", lambda e: e.dma_start(out=dbg_d["acc"], in_=FB), "d_dbg", reads=[Bn["FB"]])
                for hf in range(2):
                    cs = slice(hf * 512, (hf + 1) * 512)
                    STT(P, FB[:, cs], S.h1[:, cs], ALPHA, S.acc[hf][:, :], ALU.mult, ALU.add, [S.h1B, S.accB[hf]], [Bn["FB"]])
                layer_norm(FB, Bn["FB"], FB, Bn["FB"], g2_bc, b2_bc, stt2, mv2, sd2[:, 0:1], Bn["st2"])
                return P.dma("sp", lambda e: e.dma_start(out=out_d[s, tsl, :], in_=FB), "d_o", reads=[Bn["FB"]])

            for it in prep(0):
                P.replay(it)
            for tt in range(NT):
                nxt = prep(tt + 1) if tt + 1 < NT else None
                consume(tt, nxt)
                finalize(tt)

        last_out = None

        def full_barrier():
            toks = []
            for sname, v in P.cnt.items():
                if v > 0:
                    toks.append((sname, v))
            for eng in ("pe", "act", "dve", "pool", "sp"):
                P.wait(eng, toks)

        for s in range(nseq):
            full_barrier()
            conv_pass(s)
            full_barrier()
            attn_pass(s)
            full_barrier()
            if dbg and s == 0:
                P.op("dve", lambda e: e.tensor_copy(out=WK[:, 0:8 * SEQ], in_=mixT[:, :]))
                full_barrier()
                P.dma("sp", lambda e: e.dma_start(out=dbg_d["mixT"], in_=WK[:, 0:8 * SEQ]), "d_dbg")
                full_barrier()
            peer_phase(s)
        full_barrier()
        P.wait("sp", [(n, v) for n, v in P.cnt.items() if v > 0])
        P.emit()
    return nc


def _etab():
    slopes = 2.0 ** (-8.0 * np.arange(1, 9, dtype=np.float64) / 8.0)
    kl = np.arange(128)[:, None]
    col = np.arange(2048)[None, :]
    Dm = col - kl
    mult = ((Dm >= 0) & (Dm <= 128)).astype(np.float64) \
        + ((Dm >= 0) & (Dm % 4 == 0) & (Dm <= 512)) + ((Dm >= 0) & (Dm % 16 == 0) & (Dm <= 2048))
    E = np.zeros((8, 128, 2048), np.float64)
    for h in range(8):
        E[h] = np.exp(-slopes[h] * np.maximum(Dm, 0)) * mult
    return E.astype(np.float32)


def _dmax_blocks():
    Eb = _etab().astype(ml_dtypes.bfloat16).astype(np.float32)
    out = []
    for h in range(8):
        nz = [d for d in range(16) if np.any(Eb[h][:, d * 128:(d + 1) * 128] != 0)]
        out.append(max(nz))
    return out


DMAX_BLK = _dmax_blocks()


def make_in_maps(inputs, nseq=2, n_cores=N_CORES):
    f = lambda a: np.ascontiguousarray(np.asarray(a, dtype=np.float32))
    x = f(inputs["x"])
    w_in = f(inputs["w_in"][0]); b_in = f(inputs["b_in"][0])
    conv_w = f(inputs["conv_w"][0])
    shared = {
        "w_in": w_in,
        "bvec": np.ascontiguousarray(b_in[:2560].reshape(20, 128).T),
        "b_v": np.ascontiguousarray(b_in[2048:2560].reshape(1, 512)),
        "convw": np.ascontiguousarray(conv_w.T.reshape(4, 128, 31).transpose(1, 0, 2).reshape(128, 124)),
        "cvec": np.ascontiguousarray(np.concatenate([
            f(inputs["conv_b"][0]).reshape(4, 128).T, f(inputs["conv_ln_g"][0]).reshape(4, 128).T,
            f(inputs["conv_ln_b"][0]).reshape(4, 128).T], axis=1)),
        "w_out": f(inputs["w_out"][0]),
        "rows": np.ascontiguousarray(np.stack([f(inputs["b_out"][0]), f(inputs["ln1_g"][0]), f(inputs["ln1_b"][0]),
                                               f(inputs["ln2_g"][0]), f(inputs["ln2_b"][0])], axis=0)),
        "wq": f(inputs["peer_wq"][0]),
        "keysT": np.ascontiguousarray(f(inputs["peer_keys"][0]).reshape(16, 128, 128).transpose(2, 0, 1).reshape(128, 2048)),
        "peer_u": f(inputs["peer_u"][0]),
        "peer_v": f(inputs["peer_v"][0]),
        "etab": _etab().astype(ml_dtypes.bfloat16),
        "ident": np.eye(128, dtype=np.float32),
        "iota": np.tile(np.arange(128, dtype=np.int32)[None], (128, 1)),
        "masks": np.tile(np.array([~0x7F, 0x7F, ~0x3FFF, 0x3FFF, 0, 0, 0, 0], dtype=np.int64).astype(np.int32)[None], (128, 1)),
    }
    maps = []
    for c in range(n_cores):
        xs = x[c * nseq:(c + 1) * nseq]
        m = dict(shared)
        m["x"] = np.ascontiguousarray(xs)
        m["xT"] = np.ascontiguousarray(xs.transpose(0, 2, 1))
        maps.append(m)
    return maps


_NC_CACHE = {}


def kernel(**inputs):
    if "nc" not in _NC_CACHE:
        _NC_CACHE["nc"] = build(2)
    nc = _NC_CACHE["nc"]
    maps = make_in_maps(inputs, 2, N_CORES)
    res = run_bass_kernel_spmd(nc, maps, core_ids=list(range(N_CORES)))
    out = np.concatenate([r["out"] for r in res.results], axis=0)
    return out.astype(np.float32)
```

```python
import numpy as np
import ml_dtypes
from contextlib import ExitStack
import concourse.bass as bass
import concourse.mybir as mybir
from concourse.bass_utils import run_bass_kernel_spmd

F32 = mybir.dt.float32
BF16 = mybir.dt.bfloat16
I32 = mybir.dt.int32
U32 = mybir.dt.uint32
ALU = mybir.AluOpType
AF = mybir.ActivationFunctionType
AX = mybir.AxisListType

D = 1024
SEQ = 2048
NT = SEQ // 128
ALPHA = 2.0 ** 0.25
LN_EPS = 1e-5
NG = 14
N_CORES = 8


class Buf:
    def __init__(self, name):
        self.name = name
        self.w = None
        self.r = {}


class Prog:
    ENG = ("pe", "act", "dve", "pool", "sp")

    def __init__(self, nc, stack):
        self.nc = nc
        self.stack = stack
        self.q = {e: [] for e in self.ENG}
        self.sems = {}
        self.cnt = {}
        self.seen = {e: {} for e in self.ENG}
        self.epoch = {e: 0 for e in self.ENG}
        self.ecount = {e: 0 for e in self.ENG}
        self.pend = {e: [] for e in self.ENG}
        self.rec = None

    def replay(self, item):
        if item is None:
            return
        kind, args = item
        rec, self.rec = self.rec, None
        try:
            if kind == "op":
                self.op(*args)
            else:
                self.dma(*args)
        finally:
            self.rec = rec

    def sem(self, name):
        if name not in self.sems:
            self.sems[name] = self.stack.enter_context(self.nc.semaphore(name))
            self.cnt[name] = 0
        return self.sems[name]

    def _waits(self, eng, waits):
        for w in waits:
            if w is None:
                continue
            s, v = w
            if eng == "pe" and s.startswith("c_pe_"):
                continue
            if self.seen[eng].get(s, 0) >= v:
                continue
            self.seen[eng][s] = v
            h = self.sem(s)
            self.q[eng].append(lambda e, h=h, v=v: e.wait_ge(h, v))

    def _deps(self, reads, writes, waits):
        ws = list(waits)
        for b in reads:
            ws.append(b.w)
        for b in writes:
            ws.append(b.w)
            ws.extend(b.r.values())
        return ws

    def _finish(self, key, tok, reads, writes):
        for b in self.pend[key] if key in self.pend else ():
            b.r[key] = tok
        if key in self.pend:
            self.pend[key] = []
        for b in reads:
            b.r[key] = tok
        for b in writes:
            b.w = tok
            b.r = {}

    def op(self, eng, fn, reads=(), writes=(), mark=True, waits=()):
        if self.rec is not None:
            self.rec.append(("op", (eng, fn, list(reads), list(writes), mark, list(waits))))
            return None
        self._waits(eng, self._deps(reads, writes, waits))
        if not mark:
            self.q[eng].append(lambda e, fn=fn: fn(e))
            self.pend[eng].extend(reads)
            return None
        if self.ecount[eng] >= 6000:
            self.epoch[eng] += 1
            self.ecount[eng] = 0
        self.ecount[eng] += 1
        s = "c_%s_%d" % (eng, self.epoch[eng])
        h = self.sem(s)
        self.cnt[s] += 1
        tok = (s, self.cnt[s])
        self.q[eng].append(lambda e, fn=fn, h=h: fn(e).then_inc(h, 1))
        self._finish(eng, tok, reads, writes)
        return tok

    def dma(self, eng, fn, semname, reads=(), writes=(), waits=()):
        if self.rec is not None:
            self.rec.append(("dma", (eng, fn, semname, list(reads), list(writes), list(waits))))
            return None
        self._waits(eng, self._deps(reads, writes, waits))
        h = self.sem(semname)
        self.cnt[semname] += 16
        tok = (semname, self.cnt[semname])
        self.q[eng].append(lambda e, fn=fn, h=h: fn(e).then_inc(h, 16))
        for b in reads:
            b.r[semname] = tok
        for b in writes:
            b.w = tok
            b.r = {}
        return tok

    def wait(self, eng, waits):
        self._waits(eng, waits)

    def emit(self):
        nc = self.nc
        with nc.Block() as block:
            @block.tensor
            def _(e):
                for f in self.q["pe"]:
                    f(e)

            @block.scalar
            def _(e):
                for f in self.q["act"]:
                    f(e)

            @block.vector
            def _(e):
                for f in self.q["dve"]:
                    f(e)

            @block.gpsimd
            def _(e):
                for f in self.q["pool"]:
                    f(e)

            @block.sync
            def _(e):
                for f in self.q["sp"]:
                    f(e)


def ACT(P, out, in_, func, reads, writes, bias=None, scale=None, accum_out=None):
    kw = {}
    if bias is not None:
        kw["bias"] = bias
    if scale is not None:
        kw["scale"] = scale
    if accum_out is not None:
        kw["accum_out"] = accum_out
    return P.op("act", lambda e: e.activation(out=out, in_=in_, func=func, **kw), reads, writes)


def TT(P, out, in0, in1, op, reads, writes, eng="dve"):
    return P.op(eng, lambda e: e.tensor_tensor(out=out, in0=in0, in1=in1, op=op), reads, writes)


def TS(P, out, in0, s1, op0, reads, writes, s2=None, op1=None, eng="dve"):
    if op1 is None:
        return P.op(eng, lambda e: e.tensor_scalar(out=out, in0=in0, scalar1=s1, scalar2=None, op0=op0), reads, writes)
    return P.op(eng, lambda e: e.tensor_scalar(out=out, in0=in0, scalar1=s1, scalar2=s2, op0=op0, op1=op1), reads, writes)


def STT(P, out, in0, scalar, in1, op0, op1, reads, writes, accum_out=None):
    if accum_out is None:
        return P.op("dve", lambda e: e.scalar_tensor_tensor(out=out, in0=in0, scalar=scalar, in1=in1, op0=op0, op1=op1), reads, writes)
    return P.op("dve", lambda e: e.scalar_tensor_tensor(out=out, in0=in0, scalar=scalar, in1=in1, op0=op0, op1=op1, accum_out=accum_out), reads, writes)


def COPY(P, out, in_, reads, writes, eng="dve"):
    if eng == "act":
        return P.op("act", lambda e: e.copy(out=out, in_=in_), reads, writes)
    return P.op(eng, lambda e: e.tensor_copy(out=out, in_=in_), reads, writes)


def MM(P, out, lhsT, rhs, start, stop, reads, writes, mark):
    return P.op("pe", lambda e: e.matmul(out, lhsT, rhs, start=start, stop=stop), reads, writes, mark=mark)


def build(nseq=2, dbg=False):
    nc = bass.Bass("TRN2", target_bir_lowering=False)

    def din(name, shape, dt=F32):
        return nc.dram_tensor(name, list(shape), dt, kind="ExternalInput")

    xT_d = din("xT", [nseq, D, SEQ]).ap()
    x_d = din("x", [nseq, SEQ, D]).ap()
    w_in_d = din("w_in", [D, 2560]).ap()
    bvec_d = din("bvec", [128, 20]).ap()
    bv_h = din("b_v", [1, 512])
    convw_d = din("convw", [128, 4 * 31]).ap()
    cvec_d = din("cvec", [128, 12]).ap()
    w_out_d = din("w_out", [D, D]).ap()
    rows_h = din("rows", [5, D])
    wq_d = din("wq", [D, 2048]).ap()
    keysT_d = din("keysT", [128, 2048]).ap()
    pu_d = din("peer_u", [16384, D]).ap()
    pv_d = din("peer_v", [16384, D]).ap()
    etab_d = din("etab", [8, 128, 2048], BF16).ap()
    ident_d = din("ident", [128, 128]).ap()
    iota_d = din("iota", [128, 128], I32).ap()
    masks_d = din("masks", [128, 8], I32).ap()
    out_d = nc.dram_tensor("out", [nseq, SEQ, D], F32, kind="ExternalOutput").ap()
    uvb_d = nc.dram_tensor("uvb", [16384, 2048], BF16, kind="Internal").ap()
    wqb_d = nc.dram_tensor("wqb", [4, 128, 8 * 512], BF16, kind="Internal").ap()
    winb_d = nc.dram_tensor("winb", [D, 2560], BF16, kind="Internal").ap()
    dbg_d = {}
    if dbg:
        dbg_d["mixT"] = nc.dram_tensor("dbg_mixT", [128, 8 * SEQ], F32, kind="ExternalOutput").ap()
        dbg_d["h1"] = nc.dram_tensor("dbg_h1", [128, D], F32, kind="ExternalOutput").ap()
        dbg_d["eid"] = nc.dram_tensor("dbg_eid", [128, 128], I32, kind="ExternalOutput").ap()
        dbg_d["gate"] = nc.dram_tensor("dbg_gate", [128, 128], F32, kind="ExternalOutput").ap()
        dbg_d["acc"] = nc.dram_tensor("dbg_acc", [128, D], F32, kind="ExternalOutput").ap()

    with ExitStack() as st:
        P = Prog(nc, st)

        def sb(name, shape, dt):
            return st.enter_context(nc.sbuf_tensor("s_" + name, list(shape), dt))

        lnp = sb("lnp", [128, 5 * D], F32)
        keysT = sb("keysT", [128, 2048], BF16)
        identb = sb("identb", [128, 128], BF16)
        bvec = sb("bvec", [128, 20], F32)
        cvec = sb("cvec", [128, 12], F32)
        convw = sb("convw", [128, 4 * 31], F32)
        iota = sb("iota", [128, 128], I32)
        masks = sb("masks", [128, 8], I32)
        onesm = sb("onesm", [128, 128], F32)
        bvb = sb("bvb", [128, 512], F32)
        identf = sb("identf", [128, 128], F32)
        w_out = sb("w_out", [128, 8 * D], BF16)
        mixT = sb("mixT", [128, 8 * SEQ], BF16)
        R1 = sb("R1", [128, 12288], F32)
        WK = sb("WK", [128, 21248], F32)
        banks = [st.enter_context(nc.psum_tensor("bank%d" % i, [128, 512], F32)) for i in range(8)]
        bankB = [Buf("bank%d" % i) for i in range(8)]

        def carve(region, off_bytes, nbytes, dt):
            assert off_bytes % 4 == 0 and nbytes % 4 == 0
            a = region[:, off_bytes // 4:(off_bytes + nbytes) // 4]
            return a if dt == F32 else a.bitcast(dt)

        cB = Buf("consts")
        P.dma("sp", lambda e: e.dma_start(out=bvec[:, :], in_=bvec_d), "d_c", writes=[cB])
        P.dma("sp", lambda e: e.dma_start(out=cvec[:, :], in_=cvec_d), "d_c", writes=[cB])
        P.dma("sp", lambda e: e.dma_start(out=convw[:, :], in_=convw_d), "d_c", writes=[cB])
        P.dma("sp", lambda e: e.dma_start(out=iota[:, :], in_=iota_d), "d_c", writes=[cB])
        P.dma("sp", lambda e: e.dma_start(out=masks[:, :], in_=masks_d), "d_c", writes=[cB])
        P.dma("sp", lambda e: e.dma_start(out=identf[:, :], in_=ident_d), "d_c", writes=[cB])
        P.dma("sp", lambda e: e.dma_start(out=bvb[:, :], in_=bass.AP(bv_h, 0, [[0, 128], [1, 512]])), "d_c", writes=[cB])
        for i in range(5):
            P.dma("sp", lambda e, i=i: e.dma_start(out=lnp[:, i * D:(i + 1) * D],
                                                    in_=bass.AP(rows_h, i * D, [[0, 128], [1, D]])), "d_c", writes=[cB])
        P.dma("pool", lambda e: e.dma_start(out=identb[:, :], in_=ident_d), "d_cp", writes=[cB])
        P.dma("pool", lambda e: e.dma_start(out=keysT[:, :], in_=keysT_d), "d_cp", writes=[cB])
        woB = Buf("w_out")
        w_out_v = w_out[:, :].rearrange("p (c n) -> p c n", c=8)
        w_out_dv = w_out_d.rearrange("(c p) n -> p c n", p=128)
        for c in range(8):
            P.dma("pool", lambda e, c=c: e.dma_start(out=w_out_v[:, c, :], in_=w_out_dv[:, c, :]), "d_cp", writes=[woB])
        onesB = Buf("onesm")
        P.op("dve", lambda e: e.memset(onesm[:, :], 1.0 / 512.0), writes=[onesB])

        b_out_bc = lnp[:, 0:D]
        g1_bc, b1_bc = lnp[:, D:2 * D], lnp[:, 2 * D:3 * D]
        g2_bc, b2_bc = lnp[:, 3 * D:4 * D], lnp[:, 4 * D:5 * D]

        mixT_v = mixT[:, :].rearrange("p (c t) -> p c t", c=8)
        mixB = [Buf("mix%d" % i) for i in range(NT)]

        uvB = Buf("uvb")
        wqbB = Buf("wqb")
        wqb_v = wqb_d.rearrange("q p (c n) -> q p c n", c=8)
        winbB = Buf("winb")
        winb_dv = winb_d.rearrange("(c p) n -> p c n", p=128)

        def convert_small():
            for p_ in range(4):
                for c in range(8):
                    P.dma("pool", lambda e, p_=p_, c=c: e.dma_start(out=wqb_v[p_, :, c, :],
                                                                   in_=wq_d[c * 128:(c + 1) * 128, p_ * 512:(p_ + 1) * 512]),
                          "d_cv", writes=[wqbB])
            for c in range(8):
                for hf in range(2):
                    P.dma("pool", lambda e, c=c, hf=hf: e.dma_start(out=winb_d[c * 128:(c + 1) * 128, hf * 1280:(hf + 1) * 1280],
                                                                   in_=w_in_d[c * 128:(c + 1) * 128, hf * 1280:(hf + 1) * 1280]),
                          "d_cv", writes=[winbB])

        def convert_chunk(tab_d, col0, ch):
            r0 = ch * 2048
            P.dma("pool", lambda e: e.dma_start(out=uvb_d[r0:r0 + 2048, col0:col0 + 1024], in_=tab_d[r0:r0 + 2048, :]),
                  "d_cv", writes=[uvB])

        xT_dv = xT_d.rearrange("s (c p) t -> s p c t", p=128)
        w_in_dv = w_in_d.rearrange("(c p) n -> p c n", p=128)
        wq_dv = wq_d.rearrange("(c p) n -> p c n", p=128)

        def conv_pass(s):
            wic = carve(R1, 0, 16384, BF16).rearrange("p (c n) -> p c n", c=8)
            diag = carve(R1, 16384, 31744, BF16).rearrange("p (g n) -> p g n", n=128)
            wicB, diagB = Buf("wic"), Buf("diag")
            xts = [carve(WK, i * 8192, 8192, BF16).rearrange("p (c t) -> p c t", c=8) for i in range(2)]
            xtB = [Buf("xt0"), Buf("xt1")]
            hdn = carve(WK, 16384, 16640, BF16).rearrange("p (c t) -> p c t", c=4)
            hdnB = [Buf("hdn%d" % i) for i in range(4)]
            hpadB = Buf("hpad")
            o = 16384 + 16640
            z = carve(WK, o, 8192, F32).rearrange("p (c t) -> p c t", c=4); o += 8192
            zsq = carve(WK, o, 8192, F32).rearrange("p (c t) -> p c t", c=4); o += 8192
            sig = [carve(WK, o + i * 2048, 2048, F32) for i in range(2)]; o += 4096
            mean_sb = carve(WK, o, 2048, F32); o += 2048
            msq = carve(WK, o, 2048, F32); o += 2048
            rstd = carve(WK, o, 2048, F32); o += 2048
            t1 = [carve(WK, o + i * 2048, 2048, F32) for i in range(2)]; o += 4096
            stage = carve(WK, 64000, 8192, F32).rearrange("p (c t) -> p c t", c=8)
            stgB = Buf("stage")
            assert o <= 64000
            zB, zsqB, meanB, msqB, rstdB = Buf("z"), Buf("zsq"), Buf("mean"), Buf("msq"), Buf("rstd")
            sigB = [Buf("sig0"), Buf("sig1")]
            t1B = [Buf("t1a"), Buf("t1b")]

            for c in range(8):
                if s == 0:
                    P.dma("pool", lambda e, c=c: e.dma_start(out=wic[:, c, :], in_=w_in_dv[:, c, 0:1024]), "d_w", writes=[wicB])
                else:
                    P.dma("sp", lambda e, c=c: e.dma_start(out=wic[:, c, :], in_=winb_dv[:, c, 0:1024]), "d_w2", reads=[winbB], writes=[wicB])
            for g in range(4 * 31):
                P.op("dve", lambda e, g=g: e.tensor_scalar(out=diag[:, g, :], in0=identf[:, :], scalar1=convw[:, g:g + 1],
                                                           scalar2=None, op0=ALU.mult), reads=[cB], writes=[diagB])
            P.op("dve", lambda e: e.memset(hdn[:, :, 0:32], 0.0), writes=[hpadB])

            def xs_dma(tb_, hf_):
                P.dma("sp", lambda e: e.dma_start(
                    out=stage, in_=xT_dv[s, :, :, tb_ * 512 + hf_ * 256:tb_ * 512 + (hf_ + 1) * 256]), "d_xs", writes=[stgB])

            def xs_cast(tb_, hf_):
                COPY(P, xts[tb_ % 2][:, :, hf_ * 256:(hf_ + 1) * 256], stage, [stgB], [xtB[tb_ % 2]], eng="act")

            for tb in range(4):
                xb, xB = xts[tb % 2], xtB[tb % 2]
                if tb == 0:
                    xs_dma(0, 0)
                    xs_cast(0, 0)
                    xs_dma(0, 1)
                    xs_cast(0, 1)
                if tb + 1 < 4:
                    xs_dma(tb + 1, 0)
                if s == 0:
                    convert_chunk(pu_d, 0, 2 * tb)
                    convert_chunk(pu_d, 0, 2 * tb + 1)
                for cc in range(4):
                    pa, pg = banks[(cc % 2) * 2], banks[(cc % 2) * 2 + 1]
                    paB, pgB = bankB[(cc % 2) * 2], bankB[(cc % 2) * 2 + 1]
                    for c in range(8):
                        MM(P, pa[:, :], wic[:, c, cc * 128:(cc + 1) * 128], xb[:, c, :], c == 0, c == 7,
                           [wicB, xB], [paB], c == 7)
                    for c in range(8):
                        MM(P, pg[:, :], wic[:, c, 512 + cc * 128:512 + (cc + 1) * 128], xb[:, c, :], c == 0, c == 7,
                           [wicB, xB], [pgB], c == 7)
                    sg, sgB = sig[cc % 2], sigB[cc % 2]
                    ACT(P, sg, pg[:, :], AF.Sigmoid, [pgB, cB], [sgB], bias=bvec[:, 4 + cc:5 + cc])
                    STT(P, hdn[:, cc, 32 + tb * 512:32 + (tb + 1) * 512], pa[:, :], bvec[:, cc:cc + 1], sg,
                        ALU.add, ALU.mult, [paB, sgB, cB], [hdnB[tb]])
                    if tb + 1 < 4 and cc == 1:
                        xs_cast(tb + 1, 0)
                        xs_dma(tb + 1, 1)
                    if tb + 1 < 4 and cc == 3:
                        xs_cast(tb + 1, 1)
                for cc in range(4):
                    pc, pcB = banks[4 + cc % 2], bankB[4 + cc % 2]
                    rd = [diagB, hdnB[tb], hpadB] + ([hdnB[tb - 1]] if tb > 0 else [])
                    for k in range(31):
                        MM(P, pc[:, :], diag[:, cc * 31 + k, :], hdn[:, cc, 2 + tb * 512 + k:2 + tb * 512 + k + 512],
                           k == 0, k == 30, rd, [pcB], k == 30)
                    ACT(P, z[:, cc, :], pc[:, :], AF.Identity, [pcB, cB], [zB], bias=cvec[:, cc:cc + 1])
                    ACT(P, zsq[:, cc, :], pc[:, :], AF.Square, [pcB, cB], [zsqB], bias=cvec[:, cc:cc + 1])
                pm, pmB, pq, pqB = banks[6], bankB[6], banks[7], bankB[7]
                for cc in range(4):
                    MM(P, pm[:, :], onesm[:, :], z[:, cc, :], cc == 0, cc == 3, [onesB, zB], [pmB], cc == 3)
                for cc in range(4):
                    MM(P, pq[:, :], onesm[:, :], zsq[:, cc, :], cc == 0, cc == 3, [onesB, zsqB], [pqB], cc == 3)
                COPY(P, mean_sb, pm[:, :], [pmB], [meanB], eng="act")
                ACT(P, msq, pm[:, :], AF.Square, [pmB], [msqB])
                TT(P, rstd, pq[:, :], msq, ALU.subtract, [pqB, msqB], [rstdB])
                ACT(P, rstd, rstd, AF.Sqrt, [rstdB], [rstdB], bias=LN_EPS)
                P.op("dve", lambda e: e.reciprocal(out=rstd, in_=rstd), [rstdB], [rstdB])
                for cc in range(4):
                    tt_, ttB = t1[cc % 2], t1B[cc % 2]
                    TT(P, tt_, z[:, cc, :], mean_sb, ALU.subtract, [zB, meanB], [ttB])
                    TT(P, tt_, tt_, rstd, ALU.mult, [ttB, rstdB], [ttB])
                    ACT(P, mixT_v[:, cc, tb * 512:(tb + 1) * 512], tt_, AF.Silu, [ttB, cB], mixB[4 * tb:4 * tb + 4],
                        scale=cvec[:, 4 + cc:5 + cc], bias=cvec[:, 8 + cc:9 + cc])

        def attn_pass(s):
            wia = carve(R1, 0, 24576, BF16).rearrange("p (c n) -> p c n", c=8)
            wiaB = Buf("wia")
            Es = [carve(R1, 24576 + i * 4096, 4096, BF16) for i in range(2)]
            EB = [Buf("E0"), Buf("E1")]
            atm = carve(R1, 32768, 16384, BF16).rearrange("p (i f) -> p i f", i=16)
            atmB = [Buf("atm%d" % i) for i in range(16)]
            xts = [carve(WK, i * 8192, 8192, BF16).rearrange("p (c t) -> p c t", c=8) for i in range(2)]
            xtB = [Buf("xt0"), Buf("xt1")]
            qT = carve(WK, 16384, 16384, BF16).rearrange("p (j t) -> p j t", j=4)
            kT = carve(WK, 32768, 16384, BF16).rearrange("p (j t) -> p j t", j=4)
            Vp = carve(WK, 49152, 16896, BF16).rearrange("p (t h c) -> p t h c", t=16, h=8)
            o = 49152 + 16896
            pTs = [carve(WK, o + i * 1024, 1024, BF16) for i in range(4)]; o += 4096
            rl = [carve(WK, o + i * 4, 4, F32) for i in range(2)]; o += 8
            pTB = [Buf("pT%d" % i) for i in range(4)]
            assert o <= 70400
            stage = carve(WK, 70400, 8192, F32).rearrange("p (c t) -> p c t", c=8)
            stgB = Buf("stage")
            rlB = [Buf("rl0"), Buf("rl1")]
            qkB = [Buf("qk%d" % i) for i in range(4)]
            vB = [Buf("v%d" % i) for i in range(4)]

            for c in range(8):
                if s == 0:
                    P.dma("pool", lambda e, c=c: e.dma_start(out=wia[:, c, :], in_=w_in_dv[:, c, 1024:2560]), "d_w", writes=[wiaB])
                else:
                    P.dma("sp", lambda e, c=c: e.dma_start(out=wia[:, c, :], in_=winb_dv[:, c, 1024:2560]), "d_w2", reads=[winbB], writes=[wiaB])
            Vflat = carve(WK, 49152, 16896, BF16)
            P.op("dve", lambda e: e.memset(Vflat, 1.0), writes=vB)

            def xs_dma(tb_, hf_):
                P.dma("sp", lambda e: e.dma_start(
                    out=stage, in_=xT_dv[s, :, :, tb_ * 512 + hf_ * 256:tb_ * 512 + (hf_ + 1) * 256]), "d_xs", writes=[stgB])

            def xs_cast(tb_, hf_):
                COPY(P, xts[tb_ % 2][:, :, hf_ * 256:(hf_ + 1) * 256], stage, [stgB], [xtB[tb_ % 2]], eng="act")

            for tb in range(4):
                xb, xB = xts[tb % 2], xtB[tb % 2]
                if tb == 0:
                    xs_dma(0, 0)
                    xs_cast(0, 0)
                    xs_dma(0, 1)
                    xs_cast(0, 1)
                if tb + 1 < 4:
                    xs_dma(tb + 1, 0)
                if s == 0:
                    convert_chunk(pv_d, 1024, 2 * tb)
                    convert_chunk(pv_d, 1024, 2 * tb + 1)
                    if tb == 1:
                        convert_small()
                for j in range(8):
                    pb, pbB = banks[j % 2], bankB[j % 2]
                    for c in range(8):
                        MM(P, pb[:, :], wia[:, c, j * 128:(j + 1) * 128], xb[:, c, :], c == 0, c == 7, [wiaB, xB], [pbB], c == 7)
                    dst = (qT if j < 4 else kT)[:, j % 4, tb * 512:(tb + 1) * 512]
                    if j % 2 == 0:
                        ACT(P, dst, pb[:, :], AF.Identity, [pbB, cB], [qkB[tb]], bias=bvec[:, 8 + j:9 + j])
                    else:
                        TS(P, dst, pb[:, :], bvec[:, 8 + j:9 + j], ALU.add, [pbB, cB], [qkB[tb]])
                    if tb + 1 < 4 and j == 3:
                        xs_cast(tb + 1, 0)
                        xs_dma(tb + 1, 1)
                    if tb + 1 < 4 and j == 7:
                        xs_cast(tb + 1, 1)
                for tt in range(4):
                    tile = tb * 4 + tt
                    pb, pbB = banks[2 + tt % 2], bankB[2 + tt % 2]
                    for c in range(8):
                        MM(P, pb[:, :], xb[:, c, tt * 128:(tt + 1) * 128], wia[:, c, 1024:1536], c == 0, c == 7, [wiaB, xB], [pbB], c == 7)
                    TT(P, Vp[:, tile, :, 0:64], pb[:, :].rearrange("p (h c) -> p h c", h=8),
                       bvb[:, :].rearrange("p (h c) -> p h c", h=8), ALU.add, [pbB, cB], [vB[tb]])

            dmax = DMAX_BLK
            groups = [(h, I, g) for h in range(8) for I in range(16) for g in range((min(I, dmax[h]) + 4) // 4)]
            n_g = len(groups)

            def load_E(h):
                Eb, EbB = Es[h % 2], EB[h % 2]
                P.dma("sp", lambda e, h=h, Eb=Eb: e.dma_start(out=Eb, in_=etab_d[h, :, :]), "d_e%d" % (h % 2), writes=[EbB])

            def geo(i):
                h, I, g = groups[i]
                d0 = 4 * g
                nd = min(4, min(I, dmax[h]) + 1 - d0)
                return h, I, g, d0, nd, slice((h % 2) * 64, (h % 2) * 64 + 64), h // 2

            def stageA(i):
                h, I, g, d0, nd, pr, j = geo(i)
                ps_, psB = banks[2 + i % 4], bankB[2 + i % 4]
                for s_ in range(nd):
                    J = I - (d0 + s_)
                    MM(P, ps_[:, s_ * 128:(s_ + 1) * 128], kT[pr, j, J * 128:(J + 1) * 128],
                       qT[pr, j, I * 128:(I + 1) * 128], True, True, [qkB[J // 4], qkB[I // 4]], [psB], s_ == nd - 1)

            def stageB(i):
                h, I, g, d0, nd, pr, j = geo(i)
                ps_, psB = banks[2 + i % 4], bankB[2 + i % 4]
                pt, ptB = pTs[i % 4], pTB[i % 4]
                Eb, EbB = Es[h % 2], EB[h % 2]
                ACT(P, pt[:, 0:nd * 128], ps_[:, 0:nd * 128], AF.Exp, [psB], [ptB], scale=0.125)
                TT(P, pt[:, 0:nd * 128], pt[:, 0:nd * 128], Eb[:, d0 * 128:(d0 + nd) * 128], ALU.mult, [ptB, EbB], [ptB])
                if I == 0 and g == 0 and h + 1 < 8:
                    load_E(h + 1)

            def stageC(i):
                h, I, g, d0, nd, pr, j = geo(i)
                ngr = (min(I, dmax[h]) + 4) // 4
                pi = h * 16 + I
                po, poB = banks[6 + pi % 2], bankB[6 + pi % 2]
                pt, ptB = pTs[i % 4], pTB[i % 4]
                for s_ in range(nd):
                    J = I - (d0 + s_)
                    first = (g == 0 and s_ == 0)
                    last = (g == ngr - 1 and s_ == nd - 1)
                    MM(P, po[:, 0:65], pt[:, s_ * 128:(s_ + 1) * 128], Vp[:, J, h, 0:65], first, last,
                       [ptB, vB[J // 4]], [poB], last or s_ == nd - 1)
                if g == ngr - 1:
                    r_, rB = rl[pi % 2], rlB[pi % 2]
                    P.op("dve", lambda e, r_=r_, po=po: e.reciprocal(out=r_, in_=po[:, 64:65]), [poB], [rB])
                    TS(P, atm[:, I, h * 64:(h + 1) * 64], po[:, 0:64], r_, ALU.mult, [poB, rB], [atmB[I]])

            load_E(0)
            stageA(0)
            stageA(1)
            for i in range(n_g):
                if i + 2 < n_g:
                    stageA(i + 2)
                stageB(i)
                stageC(i)
            for I in range(16):
                pb, pbB = banks[I % 2], bankB[I % 2]
                pbb = pb[:, :].bitcast(BF16)
                for fc in range(4):
                    P.op("pe", lambda e, pbb=pbb, I=I, fc=fc: e.transpose(pbb[:, fc * 128:(fc + 1) * 128], atm[:, I, fc * 128:(fc + 1) * 128], identb[:, :]),
                         [atmB[I], cB], [pbB], mark=(fc == 3))
                COPY(P, mixT_v[:, 4:8, I * 128:(I + 1) * 128], pbb[:, 0:512].rearrange("p (c t) -> p c t", c=4),
                     [pbB], [mixB[I]], eng=("act" if I % 2 == 0 else "dve"))

        def layer_norm(src, srcB, dst, dstB, g_bc, b_bc, stt, mv, sd, stB):
            P.op("dve", lambda e: e.bn_stats(out=stt[:, 0:6], in_=src[:, 0:512]), [srcB], [stB])
            P.op("dve", lambda e: e.bn_stats(out=stt[:, 6:12], in_=src[:, 512:1024]), [srcB], [stB])
            P.op("dve", lambda e: e.bn_aggr(out=mv[:, 0:2], in_=stt[:, 0:12]), [stB], [stB])
            ACT(P, sd, mv[:, 1:2], AF.Sqrt, [stB], [stB], bias=LN_EPS)
            P.op("dve", lambda e: e.reciprocal(out=sd, in_=sd), [stB], [stB])
            TS(P, dst, src, mv[:, 0:1], ALU.subtract, [srcB, stB], [dstB], s2=sd, op1=ALU.mult)
            TT(P, dst, dst, g_bc, ALU.mult, [dstB, cB], [dstB])
            TT(P, dst, dst, b_bc, ALU.add, [dstB, cB], [dstB])

        def peer_phase(s):
            wqr = carve(R1, 0, 8192, BF16).rearrange("p (c n) -> p c n", c=8)
            wqrB = Buf("wqr")

            class St:
                pass
            states = []
            for par in range(2):
                S = St()
                base = 8192 + par * 8192
                S.h1 = carve(R1, base, 4096, F32); S.h1b = carve(R1, base + 4096, 2048, BF16)
                S.eid = carve(R1, base + 6144, 512, I32); S.gate = carve(R1, base + 6656, 512, F32)
                S.dots = carve(R1, base + 7168, 512, F32); S.w = carve(R1, base + 7680, 512, F32)
                S.h1B, S.h1bB, S.eidB, S.gateB, S.dotsB, S.wB = (Buf("h1_%d" % par), Buf("h1b_%d" % par), Buf("eid_%d" % par),
                                                                  Buf("gate_%d" % par), Buf("dots_%d" % par), Buf("w_%d" % par))
                S.acc = (banks[4 + 2 * par], banks[5 + 2 * par])
                S.accB = (bankB[4 + 2 * par], bankB[5 + 2 * par])
                states.append(S)
            o = 0

            def wk(nbytes, dt):
                nonlocal o
                a_ = carve(WK, o, nbytes, dt)
                o += nbytes
                return a_
            XR = wk(8192, F32)
            xt_, r_ = XR[:, 0:1024], XR[:, 1024:2048]
            C = XR
            h1T = wk(2048, BF16).rearrange("p (c t) -> p c t", c=8)
            qTs = wk(4096, BF16).rearrange("p (g t) -> p g t", g=16)
            A = wk(8192, F32); Bq = wk(8192, F32)
            tmp = wk(1024, F32); v1 = wk(1024, F32); i1 = wk(1024, I32); i1f = wk(1024, F32); i1s = wk(512, F32)
            wv = wk(512, F32); bv = wk(512, F32); ex = wk(512, F32); actv = wk(512, F32)
            se = wk(64, F32); stt = wk(48, F32); mv = wk(8, F32); sd = wk(8, F32)
            scr = wk(2048, BF16)
            FB = wk(4096, F32)
            stt2 = wk(48, F32); mv2 = wk(8, F32); sd2 = wk(8, F32)
            dg = [wk(256, BF16) for _ in range(8)]
            prod = [wk(2048, BF16) for _ in range(2)]
            prodB = [Buf("prod0"), Buf("prod1")]
            o = (o + 63) // 64 * 64
            nwk = (21248 * 4 - o) // 4096
            G = [wk(4096, BF16) for _ in range(nwk)]
            assert o <= 21248 * 4, o
            G += [carve(R1, 24576 + i * 4096, 4096, BF16) for i in range(NG - nwk)]
            assert 24576 + (NG - nwk) * 4096 <= 49152
            Bn = {n: Buf(n) for n in ["XR", "h1T", "qTs", "A", "Bq", "tmp", "v1", "i1", "i1f", "i1s", "wv", "bv", "ex", "se", "actv", "st", "FB", "st2"]}
            GB = [Buf("G%d" % i) for i in range(NG)]
            dgB = [Buf("dg%d" % i) for i in range(8)]
            keysT_v = keysT[:, :].rearrange("p (g n) -> p g n", g=16)
            cnt = {"gi": 0, "dg": 0, "pr": 0}

            def prep(tt):
                P.rec = []

                def idle(n):
                    P.rec.extend([None] * n)
                S = states[tt % 2]
                tsl = slice(tt * 128, (tt + 1) * 128)
                first_dbg = dbg and s == 0 and tt == 0
                P.dma("sp", lambda e: e.dma_start(out=xt_, in_=x_d[s, tsl, :]), "d_x", writes=[Bn["XR"]])
                py = [banks[0], banks[1]]
                for hf in range(2):
                    for fc in range(8):
                        MM(P, py[hf][:, :], mixT_v[:, fc, tsl], w_out_v[:, fc, hf * 512:(hf + 1) * 512], fc == 0, fc == 7,
                           [mixB[tt], woB], [bankB[hf]], fc == 7)
                idle(20)
                for hf in range(2):
                    cs = slice(hf * 512, (hf + 1) * 512)
                    STT(P, r_[:, cs], xt_[:, cs], ALPHA, py[hf][:, :], ALU.mult, ALU.add, [Bn["XR"], bankB[hf]], [Bn["XR"]])
                TT(P, r_, r_, b_out_bc, ALU.add, [Bn["XR"], cB], [Bn["XR"]])
                P.op("dve", lambda e: e.bn_stats(out=stt[:, 0:6], in_=r_[:, 0:512]), [Bn["XR"]], [Bn["st"]])
                P.op("dve", lambda e: e.bn_stats(out=stt[:, 6:12], in_=r_[:, 512:1024]), [Bn["XR"]], [Bn["st"]])
                P.op("dve", lambda e: e.bn_aggr(out=mv[:, 0:2], in_=stt[:, 0:12]), [Bn["st"]], [Bn["st"]])
                idle(4)
                ACT(P, sd[:, 0:1], mv[:, 1:2], AF.Sqrt, [Bn["st"]], [Bn["st"]], bias=LN_EPS)
                idle(4)
                P.op("dve", lambda e: e.reciprocal(out=sd[:, 0:1], in_=sd[:, 0:1]), [Bn["st"]], [Bn["st"]])
                TS(P, S.h1, r_, mv[:, 0:1], ALU.subtract, [Bn["XR"], Bn["st"]], [S.h1B], s2=sd[:, 0:1], op1=ALU.mult)
                TT(P, S.h1, S.h1, g1_bc, ALU.mult, [S.h1B, cB], [S.h1B])
                TT(P, S.h1, S.h1, b1_bc, ALU.add, [S.h1B, cB], [S.h1B])
                if first_dbg:
                    P.dma("sp", lambda e: e.dma_start(out=dbg_d["h1"], in_=S.h1), "d_dbg", reads=[S.h1B])
                idle(4)
                COPY(P, S.h1b, S.h1, [S.h1B], [S.h1bB], eng="act")
                idle(4)
                pt_ = banks[3][:, :].bitcast(BF16)
                for c in range(8):
                    P.op("pe", lambda e, c=c: e.transpose(pt_[:, c * 128:(c + 1) * 128], S.h1b[:, c * 128:(c + 1) * 128], identb[:, :]),
                         [S.h1bB, cB], [bankB[3]], mark=(c == 7))
                idle(4)
                COPY(P, h1T, pt_[:, :].rearrange("p (c t) -> p c t", c=8), [bankB[3]], [Bn["h1T"]], eng="act")
                idle(4)
                for p_ in range(4):
                    bk = 2 + p_ % 2
                    P.dma("sp", lambda e, p_=p_: e.dma_start(out=wqr, in_=wqb_v[p_]), "d_wq", reads=[wqbB], writes=[wqrB])
                    for gq in range(4 * p_, 4 * p_ + 4):
                        for c in range(8):
                            MM(P, banks[bk][:, (gq % 4) * 128:(gq % 4 + 1) * 128], wqr[:, c, (gq % 4) * 128:(gq % 4 + 1) * 128], h1T[:, c, :],
                               c == 0, c == 7, [wqrB, Bn["h1T"]], [bankB[bk]], (c == 7 and gq % 4 == 3))
                    idle(3)
                    COPY(P, qTs[:, 4 * p_:4 * p_ + 4, :], banks[bk][:, :].rearrange("p (g t) -> p g t", g=4), [bankB[bk]], [Bn["qTs"]], eng="act")
                idle(4)
                for p_ in range(4):
                    bk = 2 + p_ % 2
                    for gq in range(4 * p_, 4 * p_ + 4):
                        MM(P, banks[bk][:, (gq % 4) * 128:(gq % 4 + 1) * 128], qTs[:, gq, :], keysT_v[:, gq, :], True, True,
                           [Bn["qTs"], cB], [bankB[bk]], gq % 4 == 3)
                    idle(3)
                    COPY(P, A[:, p_ * 512:(p_ + 1) * 512], banks[bk][:, :], [bankB[bk]], [Bn["A"]], eng="act")
                idle(4)
                STT(P, Bq.bitcast(I32).rearrange("p (g n) -> p g n", g=16), A.bitcast(I32).rearrange("p (g n) -> p g n", g=16),
                    masks[:, 0:1], iota[:, :].unsqueeze(1).to_broadcast([128, 16, 128]), ALU.bitwise_and, ALU.bitwise_or,
                    [Bn["A"], cB], [Bn["Bq"]])
                for gq in range(16):
                    src = Bq[:, gq * 128:(gq + 1) * 128]
                    P.op("dve", lambda e, gq=gq, src=src: e.max(out=v1[:, gq * 16:gq * 16 + 8], in_=src), [Bn["Bq"]], [Bn["v1"]])
                    P.op("dve", lambda e, gq=gq, src=src: e.match_replace(out=tmp[:, 0:128], in_to_replace=v1[:, gq * 16:gq * 16 + 8],
                                                                         in_values=src, imm_value=-1e30), [Bn["Bq"], Bn["v1"]], [Bn["tmp"]])
                    P.op("dve", lambda e, gq=gq: e.max(out=v1[:, gq * 16 + 8:gq * 16 + 16], in_=tmp[:, 0:128]), [Bn["tmp"]], [Bn["v1"]])
                TS(P, i1, v1.bitcast(I32), masks[:, 1:2], ALU.bitwise_and, [Bn["v1"], cB], [Bn["i1"]])
                COPY(P, i1f, i1, [Bn["i1"]], [Bn["i1f"]])
                v1v = v1.rearrange("p (h q a) -> p h q a", h=8, q=2)
                i1fv = i1f.rearrange("p (h q a) -> p h q a", h=8, q=2)
                TS(P, i1s.rearrange("p (h a) -> p h a", h=8), i1fv[:, :, 0, :], 128.0, ALU.mult, [Bn["i1f"]], [Bn["i1s"]])
                candv = A.rearrange("p (h a b) -> p h a b", h=8, a=16)
                TT(P, candv, v1v[:, :, 0, :].unsqueeze(3).to_broadcast([128, 8, 16, 16]),
                   v1v[:, :, 1, :].unsqueeze(2).to_broadcast([128, 8, 16, 16]), ALU.add, [Bn["v1"], Bn["A"]], [Bn["A"]])
                TT(P, C.rearrange("p (h a b) -> p h a b", h=8, a=16),
                   i1s.rearrange("p (h a) -> p h a", h=8).unsqueeze(3).to_broadcast([128, 8, 16, 16]),
                   i1fv[:, :, 1, :].unsqueeze(2).to_broadcast([128, 8, 16, 16]), ALU.add, [Bn["i1s"], Bn["i1f"]], [Bn["XR"]])
                COPY(P, C.bitcast(I32), C, [Bn["XR"]], [Bn["XR"]])
                STT(P, Bq.bitcast(I32), A.bitcast(I32), masks[:, 2:3], C.bitcast(I32), ALU.bitwise_and, ALU.bitwise_or,
                    [Bn["A"], Bn["XR"], cB], [Bn["Bq"]])
                for hh in range(8):
                    for (srcbuf, srcn, dstbuf, dstn) in ((Bq, "Bq", wv, "wv"), (A, "A", bv, "bv")):
                        src = srcbuf[:, hh * 256:(hh + 1) * 256]
                        P.op("dve", lambda e, hh=hh, src=src, dstbuf=dstbuf: e.max(out=dstbuf[:, hh * 16:hh * 16 + 8], in_=src),
                             [Bn[srcn]], [Bn[dstn]])
                        P.op("dve", lambda e, hh=hh, src=src, dstbuf=dstbuf: e.match_replace(
                            out=tmp[:, 0:256], in_to_replace=dstbuf[:, hh * 16:hh * 16 + 8], in_values=src, imm_value=-1e30),
                            [Bn[srcn], Bn[dstn]], [Bn["tmp"]])
                        P.op("dve", lambda e, hh=hh, dstbuf=dstbuf: e.max(out=dstbuf[:, hh * 16 + 8:hh * 16 + 16], in_=tmp[:, 0:256]),
                             [Bn["tmp"]], [Bn[dstn]])
                TS(P, S.eid, wv.bitcast(I32), masks[:, 3:4], ALU.bitwise_and, [Bn["wv"], cB], [S.eidB])
                bvv = bv.rearrange("p (h k) -> p h k", h=8)
                TT(P, ex.rearrange("p (h k) -> p h k", h=8), bvv, bvv[:, :, 0:1].to_broadcast([128, 8, 16]), ALU.subtract,
                   [Bn["bv"]], [Bn["ex"]])
                idle(3)
                ACT(P, ex, ex, AF.Exp, [Bn["ex"]], [Bn["ex"]])
                idle(3)
                P.op("dve", lambda e: e.tensor_reduce(out=se[:, 0:8], in_=ex.rearrange("p (h k) -> p h k", h=8), axis=AX.X, op=ALU.add),
                     [Bn["ex"]], [Bn["se"]])
                P.op("dve", lambda e: e.reciprocal(out=se[:, 0:8], in_=se[:, 0:8]), [Bn["se"]], [Bn["se"]])
                TT(P, S.gate.rearrange("p (h k) -> p h k", h=8), ex.rearrange("p (h k) -> p h k", h=8),
                   se[:, 0:8].unsqueeze(2).to_broadcast([128, 8, 16]), ALU.mult, [Bn["ex"], Bn["se"]], [S.gateB])
                if first_dbg:
                    P.dma("sp", lambda e: e.dma_start(out=dbg_d["eid"], in_=S.eid), "d_dbg", reads=[S.eidB])
                    P.dma("sp", lambda e: e.dma_start(out=dbg_d["gate"], in_=S.gate), "d_dbg", reads=[S.gateB])
                rec, P.rec = P.rec, None
                return rec

            def half_a(S, g, dot_tok):
                g4 = slice(g * 4, g * 4 + 4)
                t_g = P.op("act", lambda e: e.activation(out=actv[:, g4], in_=S.dots[:, g4], func=AF.Gelu), [], [Bn["actv"]], waits=dot_tok)
                S.dotsB.r["act"] = t_g

            def half_b(S, g, slotbuf):
                g4 = slice(g * 4, g * 4 + 4)
                TT(P, S.w[:, g4], actv[:, g4], S.gate[:, g4], ALU.mult, [Bn["actv"], S.gateB], [S.wB])
                for j in range(4):
                    sl = g * 4 + j
                    b = slotbuf[sl]
                    k = cnt["dg"] % 8
                    cnt["dg"] += 1
                    ACT(P, dg[k], identf[:, :], AF.Copy, [S.wB, cB], [dgB[k]], scale=S.w[:, sl:sl + 1])
                    for hf in range(2):
                        MM(P, S.acc[hf][:, :], dg[k], G[b][:, 1024 + hf * 512:1024 + (hf + 1) * 512], sl == 0, sl == 127,
                           [dgB[k], GB[b]], [S.accB[hf]], hf == 1 or sl == 127)

            def consume(tt, nxt):
                S = states[tt % 2]
                slotbuf = {}
                pending = None
                for g in range(32):
                    if pending is not None:
                        half_a(S, pending[0], pending[1])
                    toks = []
                    for j in range(4):
                        sl = g * 4 + j
                        b = cnt["gi"] % NG
                        cnt["gi"] += 1
                        slotbuf[sl] = b
                        P.dma("pool", lambda e, b=b, sl=sl: e.indirect_dma_start(
                            out=G[b], out_offset=None, in_=uvb_d,
                            in_offset=bass.IndirectOffsetOnAxis(ap=S.eid[:, sl:sl + 1].bitcast(U32), axis=0)),
                            "d_g%d" % b, reads=[S.eidB, uvB], writes=[GB[b]])
                        if j % 2 == 1:
                            tok = STT(P, scr, G[b][:, 0:1024], 1.0, S.h1b, ALU.mult, ALU.mult, [GB[b], S.h1bB],
                                      ([S.dotsB] if sl == 0 else []), accum_out=S.dots[:, sl:sl + 1])
                        else:
                            kp = cnt["pr"] % 2
                            cnt["pr"] += 1
                            TT(P, prod[kp], G[b][:, 0:1024], S.h1b, ALU.mult, [GB[b], S.h1bB], [prodB[kp]])
                            tok = P.op("act", lambda e, kp=kp, sl=sl: e.activation(out=prod[kp], in_=prod[kp], func=AF.Copy,
                                                                                   accum_out=S.dots[:, sl:sl + 1]),
                                       [prodB[kp]], [prodB[kp]] + ([S.dotsB] if sl == 0 else []))
                        toks.append(tok)
                        if j == 1 and pending is not None:
                            half_b(S, pending[0], slotbuf)
                        n_tot = n_dve = 0
                        while nxt and n_tot < 5 and n_dve < 2:
                            it = nxt.pop(0)
                            n_tot += 1
                            if it is not None and it[1][0] == "dve":
                                n_dve += 1
                            P.replay(it)
                    pending = (g, toks)
                half_a(S, pending[0], pending[1])
                half_b(S, pending[0], slotbuf)
                while nxt:
                    P.replay(nxt.pop(0))

            def finalize(tt):
                S = states[tt % 2]
                tsl = slice(tt * 128, (tt + 1) * 128)
                if dbg and s == 0 and tt == 0:
                    for hf in range(2):
                        COPY(P, FB[:, hf * 512:(hf + 1) * 512], S.acc[hf][:, :], [S.accB[hf]], [Bn["FB"]])
                    P.dma("sp", lambda e: e.dma_start(out=dbg_d["acc"], in_=FB), "d_dbg", reads=[Bn["FB"]])
                for hf in range(2):
                    cs = slice(hf * 512, (hf + 1) * 512)
                    STT(P, FB[:, cs], S.h1[:, cs], ALPHA, S.acc[hf][:, :], ALU.mult, ALU.add, [S.h1B, S.accB[hf]], [Bn["FB"]])
                layer_norm(FB, Bn["FB"], FB, Bn["FB"], g2_bc, b2_bc, stt2, mv2, sd2[:, 0:1], Bn["st2"])
                return P.dma("sp", lambda e: e.dma_start(out=out_d[s, tsl, :], in_=FB), "d_o", reads=[Bn["FB"]])

            for it in prep(0):
                P.replay(it)
            for tt in range(NT):
                nxt = prep(tt + 1) if tt + 1 < NT else None
                consume(tt, nxt)
                finalize(tt)

        last_out = None

        def full_barrier():
            toks = []
            for sname, v in P.cnt.items():
                if v > 0:
                    toks.append((sname, v))
            for eng in ("pe", "act", "dve", "pool", "sp"):
                P.wait(eng, toks)

        for s in range(nseq):
            full_barrier()
            conv_pass(s)
            full_barrier()
            attn_pass(s)
            full_barrier()
            if dbg and s == 0:
                P.op("dve", lambda e: e.tensor_copy(out=WK[:, 0:8 * SEQ], in_=mixT[:, :]))
                full_barrier()
                P.dma("sp", lambda e: e.dma_start(out=dbg_d["mixT"], in_=WK[:, 0:8 * SEQ]), "d_dbg")
                full_barrier()
            peer_phase(s)
        full_barrier()
        P.wait("sp", [(n, v) for n, v in P.cnt.items() if v > 0])
        P.emit()
    return nc


def _etab():
    slopes = 2.0 ** (-8.0 * np.arange(1, 9, dtype=np.float64) / 8.0)
    kl = np.arange(128)[:, None]
    col = np.arange(2048)[None, :]
    Dm = col - kl
    mult = ((Dm >= 0) & (Dm <= 128)).astype(np.float64) \
        + ((Dm >= 0) & (Dm % 4 == 0) & (Dm <= 512)) + ((Dm >= 0) & (Dm % 16 == 0) & (Dm <= 2048))
    E = np.zeros((8, 128, 2048), np.float64)
    for h in range(8):
        E[h] = np.exp(-slopes[h] * np.maximum(Dm, 0)) * mult
    return E.astype(np.float32)


def _dmax_blocks():
    Eb = _etab().astype(ml_dtypes.bfloat16).astype(np.float32)
    out = []
    for h in range(8):
        nz = [d for d in range(16) if np.any(Eb[h][:, d * 128:(d + 1) * 128] != 0)]
        out.append(max(nz))
    return out


DMAX_BLK = _dmax_blocks()


def make_in_maps(inputs, nseq=2, n_cores=N_CORES):
    f = lambda a: np.ascontiguousarray(np.asarray(a, dtype=np.float32))
    x = f(inputs["x"])
    w_in = f(inputs["w_in"][0]); b_in = f(inputs["b_in"][0])
    conv_w = f(inputs["conv_w"][0])
    shared = {
        "w_in": w_in,
        "bvec": np.ascontiguousarray(b_in[:2560].reshape(20, 128).T),
        "b_v": np.ascontiguousarray(b_in[2048:2560].reshape(1, 512)),
        "convw": np.ascontiguousarray(conv_w.T.reshape(4, 128, 31).transpose(1, 0, 2).reshape(128, 124)),
        "cvec": np.ascontiguousarray(np.concatenate([
            f(inputs["conv_b"][0]).reshape(4, 128).T, f(inputs["conv_ln_g"][0]).reshape(4, 128).T,
            f(inputs["conv_ln_b"][0]).reshape(4, 128).T], axis=1)),
        "w_out": f(inputs["w_out"][0]),
        "rows": np.ascontiguousarray(np.stack([f(inputs["b_out"][0]), f(inputs["ln1_g"][0]), f(inputs["ln1_b"][0]),
                                               f(inputs["ln2_g"][0]), f(inputs["ln2_b"][0])], axis=0)),
        "wq": f(inputs["peer_wq"][0]),
        "keysT": np.ascontiguousarray(f(inputs["peer_keys"][0]).reshape(16, 128, 128).transpose(2, 0, 1).reshape(128, 2048)),
        "peer_u": f(inputs["peer_u"][0]),
        "peer_v": f(inputs["peer_v"][0]),
        "etab": _etab().astype(ml_dtypes.bfloat16),
        "ident": np.eye(128, dtype=np.float32),
        "iota": np.tile(np.arange(128, dtype=np.int32)[None], (128, 1)),
        "masks": np.tile(np.array([~0x7F, 0x7F, ~0x3FFF, 0x3FFF, 0, 0, 0, 0], dtype=np.int64).astype(np.int32)[None], (128, 1)),
    }
    maps = []
    for c in range(n_cores):
        xs = x[c * nseq:(c + 1) * nseq]
        m = dict(shared)
        m["x"] = np.ascontiguousarray(xs)
        m["xT"] = np.ascontiguousarray(xs.transpose(0, 2, 1))
        maps.append(m)
    return maps


_NC_CACHE = {}


def kernel(**inputs):
    if "nc" not in _NC_CACHE:
        _NC_CACHE["nc"] = build(2)
    nc = _NC_CACHE["nc"]
    maps = make_in_maps(inputs, 2, N_CORES)
    res = run_bass_kernel_spmd(nc, maps, core_ids=list(range(N_CORES)))
    out = np.concatenate([r["out"] for r in res.results], axis=0)
    return out.astype(np.float32)
```
